# Optimizing a Trainium2 kernel written in Bass

```python
import jax, jax.numpy as jnp
from jax import lax
import numpy as np

D_MODEL = 1024
BATCH = 8
SEQ = 4096
DEPTH = 2

N_MEM = 256
XA_HEADS = 4
XA_HD = D_MODEL // XA_HEADS
W_A = D_MODEL
H_A = 8
HD_A = W_A // H_A
CONV_A = 4
C_RG = 8.0
W_B = D_MODEL // 2
POOL_WINDOWS = (2, 4, 8, 16)
G_B = len(POOL_WINDOWS)
HD_B = W_B // G_B
IN_AB = 2 * W_A + W_B
OUT_AB = W_A + W_B
CONV_C = 31
D_FF = 3 * D_MODEL
CONV_F = 3
EPS = 1e-6
N_EVEN = (DEPTH + 1) // 2
N_ODD = DEPTH // 2

kernel_name = "hybrid_rglru_pool_conformer_xattn_convffn"


def rms_norm(x, g):
    xf = x.astype(jnp.float32)
    y = xf * lax.rsqrt(jnp.mean(xf * xf, axis=-1, keepdims=True) + EPS)
    return (y * g.astype(jnp.float32)).astype(x.dtype)


def layer_norm(x, g, b):
    xf = x.astype(jnp.float32)
    mu = jnp.mean(xf, axis=-1, keepdims=True)
    var = jnp.mean(jnp.square(xf - mu), axis=-1, keepdims=True)
    y = (xf - mu) * lax.rsqrt(var + EPS)
    return (y * g.astype(jnp.float32) + b.astype(jnp.float32)).astype(x.dtype)


def causal_dwconv(x, w, b):
    K, C = w.shape
    y = lax.conv_general_dilated(
        x, w[:, None, :], window_strides=(1,), padding=[(K - 1, 0)],
        dimension_numbers=("NWC", "WIO", "NWC"), feature_group_count=C)
    return y + b


def rg_lru(x, w_gx, b_gx, w_ga, b_ga, lam):
    Bn, S, W = x.shape
    xh = x.reshape(Bn, S, H_A, HD_A)
    gate_x = jax.nn.sigmoid(jnp.einsum('bshi,hij->bshj', xh, w_gx).reshape(Bn, S, W) + b_gx)
    gate_a = jax.nn.sigmoid(jnp.einsum('bshi,hij->bshj', xh, w_ga).reshape(Bn, S, W) + b_ga)
    log_a = -C_RG * gate_a.astype(jnp.float32) * jax.nn.softplus(-lam.astype(jnp.float32))
    a = jnp.exp(log_a)
    mult = jnp.sqrt(-jnp.expm1(2.0 * log_a))
    bx = mult * (gate_x * x).astype(jnp.float32)

    def combine(lhs, rhs):
        a_l, b_l = lhs
        a_r, b_r = rhs
        return a_l * a_r, a_r * b_l + b_r

    _, h = lax.associative_scan(combine, (a, bx), axis=1)
    return h.astype(x.dtype)


def multi_scale_pool(u, w_g, b_g, scale):
    Bn, S, W = u.shape
    uf = u.astype(jnp.float32)
    csum = jnp.cumsum(uf, axis=1)
    t = jnp.arange(1, S + 1, dtype=jnp.float32)[:, None]
    outs = []
    for g, w in enumerate(POOL_WINDOWS):
        sl = slice(g * HD_B, (g + 1) * HD_B)
        cg = csum[..., sl]
        lagged = jnp.pad(cg[:, :S - w], ((0, 0), (w, 0), (0, 0)))
        mean = (cg - lagged) / jnp.minimum(t, float(w))
        outs.append(mean - uf[..., sl])
    p = jnp.stack(outs, axis=2).astype(u.dtype)
    y = jnp.einsum('bsgi,gij->bsgj', p, w_g).reshape(Bn, S, W) + b_g
    return y * scale


def mixer_ab(x, norm, w_in, conv_w, conv_b, w_gx, b_gx, w_ga, b_ga, lam, w_pool, b_pool, pool_scale, w_out):
    z = rms_norm(x, norm) @ w_in
    z_gate = z[..., :W_A]
    z_rec = z[..., W_A:2 * W_A]
    z_pool = z[..., 2 * W_A:]
    xr = causal_dwconv(z_rec, conv_w, conv_b)
    y_a = jax.nn.gelu(z_gate) * rg_lru(xr, w_gx, b_gx, w_ga, b_ga, lam)
    y_b = multi_scale_pool(z_pool, w_pool, b_pool, pool_scale)
    return jnp.concatenate([y_a, y_b], axis=-1) @ w_out


def conformer_conv(x, norm, w1, b1, dw_w, dw_b, ln_g, ln_b, w2, b2):
    h = rms_norm(x, norm) @ w1 + b1
    h = jax.nn.glu(h, axis=-1)
    h = causal_dwconv(h, dw_w, dw_b)
    h = jax.nn.silu(layer_norm(h, ln_g, ln_b))
    return h @ w2 + b2


def cross_attn(x, mem, norm, mem_norm, wq, wk, wv, wo):
    Bn, S, _ = x.shape
    M = mem.shape[1]
    q = (rms_norm(x, norm) @ wq).reshape(Bn, S, XA_HEADS, XA_HD)
    m = rms_norm(mem, mem_norm)
    k = (m @ wk).reshape(Bn, M, XA_HEADS, XA_HD)
    v = (m @ wv).reshape(Bn, M, XA_HEADS, XA_HD)
    s = jnp.einsum('bqhd,bkhd->bhqk', q, k).astype(jnp.float32) * (XA_HD ** -0.5)
    p = jax.nn.softmax(s, axis=-1).astype(x.dtype)
    o = jnp.einsum('bhqk,bkhd->bqhd', p, v).reshape(Bn, S, D_MODEL)
    return o @ wo


def conv_ffn(x, norm, w_up, dw_w, dw_b, w_down):
    h = rms_norm(x, norm) @ w_up
    g = causal_dwconv(h[..., :D_FF], dw_w, dw_b)
    u = h[..., D_FF:]
    return (jax.nn.gelu(g) * u) @ w_down


def setup_inputs(seed: int = 0) -> dict:
    key = jax.random.key(seed)
    keys = iter(jax.random.split(key, 48))

    def nrm(shape, scale):
        return jax.random.normal(next(keys), shape, jnp.float32) * scale

    def gain(shape):
        return 1.0 + 0.02 * jax.random.normal(next(keys), shape, jnp.float32)

    L, NE, NO, D = DEPTH, N_EVEN, N_ODD, D_MODEL
    u = jax.random.uniform(next(keys), (NE, W_A), jnp.float32, 0.9, 0.999) ** (1.0 / C_RG)
    lam = jnp.log(u) - jnp.log1p(-u)
    return {
        "x": jax.random.normal(next(keys), (BATCH, SEQ, D), jnp.float32),
        "mem": jax.random.normal(next(keys), (BATCH, N_MEM, D), jnp.float32),
        "ab_norm": gain((NE, D)),
        "ab_w_in": nrm((NE, D, IN_AB), D ** -0.5),
        "a_conv_w": nrm((NE, CONV_A, W_A), CONV_A ** -0.5),
        "a_conv_b": nrm((NE, W_A), 0.01),
        "a_gate_x_w": nrm((NE, H_A, HD_A, HD_A), HD_A ** -0.5),
        "a_gate_x_b": nrm((NE, W_A), 0.01),
        "a_gate_a_w": nrm((NE, H_A, HD_A, HD_A), HD_A ** -0.5),
        "a_gate_a_b": nrm((NE, W_A), 0.01),
        "a_lambda": lam,
        "b_group_w": nrm((NE, G_B, HD_B, HD_B), HD_B ** -0.5),
        "b_group_b": nrm((NE, W_B), 0.01),
        "b_scale": 1.0 + 0.1 * jax.random.normal(next(keys), (NE, W_B), jnp.float32),
        "ab_w_out": nrm((NE, OUT_AB, D), OUT_AB ** -0.5),
        "c_norm": gain((NO, D)),
        "c_w_pw1": nrm((NO, D, 2 * D), D ** -0.5),
        "c_b_pw1": nrm((NO, 2 * D), 0.01),
        "c_dw_w": nrm((NO, CONV_C, D), CONV_C ** -0.5),
        "c_dw_b": nrm((NO, D), 0.01),
        "c_ln_g": gain((NO, D)),
        "c_ln_b": nrm((NO, D), 0.01),
        "c_w_pw2": nrm((NO, D, D), D ** -0.5),
        "c_b_pw2": nrm((NO, D), 0.01),
        "xa_norm": gain((L, D)),
        "xa_mem_norm": gain((L, D)),
        "xa_wq": nrm((L, D, D), D ** -0.5),
        "xa_wk": nrm((L, D, D), D ** -0.5),
        "xa_wv": nrm((L, D, D), D ** -0.5),
        "xa_wo": nrm((L, D, D), D ** -0.5),
        "f_norm": gain((L, D)),
        "f_w_up": nrm((L, D, 2 * D_FF), D ** -0.5),
        "f_dw_w": nrm((L, CONV_F, D_FF), CONV_F ** -0.5),
        "f_dw_b": nrm((L, D_FF), 0.01),
        "f_w_down": nrm((L, D_FF, D), D_FF ** -0.5),
        "final_norm": gain((D,)),
    }


def reference(x, mem, ab_norm, ab_w_in, a_conv_w, a_conv_b, a_gate_x_w, a_gate_x_b,
              a_gate_a_w, a_gate_a_b, a_lambda, b_group_w, b_group_b, b_scale, ab_w_out,
              c_norm, c_w_pw1, c_b_pw1, c_dw_w, c_dw_b, c_ln_g, c_ln_b, c_w_pw2, c_b_pw2,
              xa_norm, xa_mem_norm, xa_wq, xa_wk, xa_wv, xa_wo,
              f_norm, f_w_up, f_dw_w, f_dw_b, f_w_down, final_norm):
    for layer in range(DEPTH):
        if layer % 2 == 0:
            i = layer // 2
            x = x + mixer_ab(x, ab_norm[i], ab_w_in[i], a_conv_w[i], a_conv_b[i],
                             a_gate_x_w[i], a_gate_x_b[i], a_gate_a_w[i], a_gate_a_b[i],
                             a_lambda[i], b_group_w[i], b_group_b[i], b_scale[i], ab_w_out[i])
        else:
            j = layer // 2
            x = x + conformer_conv(x, c_norm[j], c_w_pw1[j], c_b_pw1[j], c_dw_w[j], c_dw_b[j],
                                   c_ln_g[j], c_ln_b[j], c_w_pw2[j], c_b_pw2[j])
        x = x + cross_attn(x, mem, xa_norm[layer], xa_mem_norm[layer], xa_wq[layer],
                           xa_wk[layer], xa_wv[layer], xa_wo[layer])
        x = x + conv_ffn(x, f_norm[layer], f_w_up[layer], f_dw_w[layer], f_dw_b[layer],
                         f_w_down[layer])
    return rms_norm(x, final_norm)
```

```python
import numpy as np
from contextlib import ExitStack
import concourse.bass as bass
import concourse.mybir as mybir
from concourse.bass_utils import run_bass_kernel_spmd

F32, BF16 = mybir.dt.float32, mybir.dt.bfloat16
AF = mybir.ActivationFunctionType
ALU = mybir.AluOpType
S = slice(None)
TT_ = 512
SEQ = 4096
NMEM = 256
CH = 32
NSLOT = 4
PIECE = 4
EPS = 1e-6
NTMP = 12
NTB = 8
MIX_STAGGER = 6
ALL_STAGES = ('mix0', 'xa0', 'ffn0', 'conf1', 'xa1', 'ffn1', 'final')


class R:
    __slots__ = ('name', 'idx', 'key')

    def __init__(self, name, *idx, key=None):
        self.name = name
        self.idx = idx
        self.key = key if key is not None else (name,) + tuple(i for i in idx if isinstance(i, int))

    def ap(self, tm):
        t = tm[self.name]
        return t[self.idx] if self.idx else t[:]

    def sub(self, a, b):
        idx = list(self.idx)
        last = idx[-1]
        start = 0 if last.start is None else last.start
        idx[-1] = slice(start + a, start + b)
        return R(self.name, *idx, key=self.key)


class Op:
    __slots__ = ('eng', 'fn', 'deps', 'dma', 'sig', 'comp')

    def __init__(self, eng, fn, deps, dma):
        self.eng, self.fn, self.deps, self.dma = eng, fn, deps, dma
        self.sig = False
        self.comp = None


class Prog:
    def __init__(self):
        self.ops = []
        self.last_w = {}
        self.readers = {}

    def op(self, eng, fn, reads=(), writes=(), dma=None):
        idx = len(self.ops)
        deps = {}
        for k in reads:
            w = self.last_w.get(k)
            if w is not None:
                deps[w] = 'RAW'
        for k in writes:
            w = self.last_w.get(k)
            if w is not None:
                deps.setdefault(w, 'WAW')
            for r in self.readers.get(k, {}).values():
                deps.setdefault(r, 'WAR')
        chan = dma if dma is not None else eng
        for k in reads:
            self.readers.setdefault(k, {})[chan] = idx
        for k in writes:
            self.last_w[k] = idx
            self.readers[k] = {}
        self.ops.append(Op(eng, fn, deps, dma))
        return idx

    def finalize(self):
        ops = self.ops
        for o in ops:
            for d, typ in o.deps.items():
                p = ops[d]
                if p.dma is not None:
                    continue
                if o.dma is not None or p.eng != o.eng or typ == 'RAW':
                    p.sig = True
        cnt = {}
        for o in ops:
            if o.dma is not None:
                cnt[o.dma] = cnt.get(o.dma, 0) + 16
                o.comp = ('dma_' + o.dma, cnt[o.dma])
            elif o.sig:
                cnt[o.eng] = cnt.get(o.eng, 0) + 1
                o.comp = ('eng_' + o.eng, cnt[o.eng])
        self.sem_names = sorted(set(o.comp[0] for o in ops if o.comp is not None))

    def emit_engine(self, engname, eng, sems):
        ops = self.ops
        waited = {}
        for o in ops:
            if o.eng != engname:
                continue
            for d, typ in o.deps.items():
                p = ops[d]
                if p.dma is None and not (o.dma is not None or p.eng != o.eng or typ == 'RAW'):
                    continue
                sname, val = p.comp
                if waited.get(sname, 0) >= val:
                    continue
                waited[sname] = val
                eng.wait_ge(sems[sname], val)
            ins = o.fn(eng)
            if o.comp is not None:
                ins.then_inc(sems[o.comp[0]], 16 if o.dma is not None else 1)
        return waited


class Builder:
    def __init__(self, nt=SEQ // TT_, stages=ALL_STAGES, known=None):
        self.nt = nt
        self.stages = tuple(stages)
        self.P = Prog()
        self.setup_blocks = []
        self.tile_blocks = []
        self.known = known
        self.pool_ok = False
        if known is not None:
            self.ns_pad = -(-len(known[0]) // CH) * CH
            self.nb_pad = -(-len(known[1]) // CH) * CH
            self.nb_real_ch = -(-len(known[1]) // CH)
            self.ncols = (self.ns_pad + self.nb_pad) * 128
        else:
            self.nb_pad = None
            self.ns_pad = None
        self.chunks_issued = set()
        self.pieces_cast = set()
        self.phase = 'setup'
        self.tile = -1
        self.bpos = 0
        self.bfree_ = [R('ps%d' % i, S, slice(0, 512)) for i in range(8)]
        self.tfree_ = [R('tmp%d' % i, S, slice(0, 512)) for i in range(NTMP)]
        self.tbfree_ = [R('tb%d' % i, S, slice(0, 512)) for i in range(NTB)]
        self.pcols = {}
        self.np_cols = 0
        self.pdcols = {}
        self.npd_cols = 0
        self._param_layout()
        self.trace()
        self.P.finalize()

    def _padd(self, name, n):
        self.pcols[name] = self.np_cols
        self.np_cols += n

    def _pdadd(self, name, n):
        self.pdcols[name] = self.npd_cols
        self.npd_cols += n

    def _param_layout(self):
        a = self._padd
        a('ab_norm', 8)
        for k in range(4):
            a('a_conv_w%d' % k, 8)
        for n in ('a_conv_b', 'a_gate_x_b', 'a_gate_a_b', 'a_lambda'):
            a(n, 8)
        a('b_group_b', 4)
        a('b_scale', 4)
        a('c_norm', 8)
        a('c_b_pw1', 16)
        for n in ('c_dw_b', 'c_ln_g', 'c_ln_b', 'c_b_pw2'):
            a(n, 8)
        for l in range(2):
            a('xa_norm%d' % l, 8)
            a('xa_mem_norm%d' % l, 8)
            a('f_norm%d' % l, 8)
            for k in range(3):
                a('f_dw_w%d_%d' % (l, k), 24)
            a('f_dw_b%d' % l, 24)
        a('final_norm', 8)
        a('pf', 64)
        for n in ('ngxb', 'ngab', 'ncb1', 'cvec', 'c2vec', 's0', 's1', 'lnq'):
            self._pdadd(n, 8)

    def Pc(self, name, i=0, n=1):
        c = self.pcols[name] + i
        return R('P', S, slice(c, c + n), key=('P',))

    def PDc(self, name, i=0, n=1):
        c = self.pdcols[name] + i
        return R('PD', S, slice(c, c + n), key=('PD', name))

    def balloc(self, n=512):
        assert self.bfree_, 'PSUM banks exhausted'
        b = self.bfree_.pop(0)
        return b if n == 512 else b.sub(0, n)

    def bfree(self, b):
        self.bfree_.append(R(b.name, S, slice(0, 512)))

    def talloc(self, n=512):
        assert self.tfree_, 'tmp pool exhausted'
        t = self.tfree_.pop(0)
        return t if n == 512 else t.sub(0, n)

    def tfree(self, t):
        idx = list(t.idx)
        idx[-1] = slice(0, 512)
        self.tfree_.append(R(t.name, *idx))

    def tballoc(self):
        assert self.tbfree_, 'bf16 tmp pool exhausted'
        return self.tbfree_.pop(0)

    def tbfree(self, t):
        self.tbfree_.append(R(t.name, S, slice(0, 512)))

    def pipeline(self, gens, stagger, triggers=None):
        active, nxt, rnd = [], 0, 0
        triggers = triggers or {}
        while nxt < len(gens) or active:
            if nxt < len(gens) and rnd >= nxt * stagger:
                active.append(gens[nxt])
                nxt += 1
            for g in list(active):
                try:
                    next(g)
                except StopIteration:
                    active.remove(g)
                    active.extend(triggers.get(id(g), []))
            rnd += 1

    def W(self, *spec, nblk=1):
        if self.phase == 'setup':
            pos = len(self.setup_blocks)
            for j in range(nblk):
                self.setup_blocks.append(spec + ((j,) if nblk > 1 else ()))
            gb = pos
        else:
            pos = self.bpos
            for j in range(nblk):
                s = spec + ((j,) if nblk > 1 else ())
                if self.tile == 0:
                    self.tile_blocks.append(s)
                else:
                    assert self.tile_blocks[pos + j] == s
            self.bpos += nblk
            nbp = self.nb_pad if (self.tile > 0) else 0
            gb = self.ns_pad + self.tile * nbp + pos
        g0, g1 = gb // CH, (gb + nblk - 1) // CH
        assert g0 == g1
        self._ensure_chunk(g0)
        slot = g0 % NSLOT
        off = (slot * CH + gb % CH) * 128
        return R('ring', S, slice(off, off + 128 * nblk), key=('ring', slot))

    def _ensure_chunk(self, g, lookahead=2):
        if g in self.chunks_issued:
            return
        self.chunks_issued.add(g)
        slot = g % NSLOT
        if self.known is None:
            self.P.op('sp', lambda e: None, reads=[], writes=[('ring', slot)], dma='ring%d' % slot)
            return
        nsc = self.ns_pad // CH
        nch = self.nb_pad // CH
        first_pass = g < nsc + nch
        if first_pass:
            dcol = g * CH * 128

            def ldfn(e, dcol=dcol, slot=slot):
                return e.dma_start(
                    out=self.tm['ring'][:, slot * CH * 128:(slot + 1) * CH * 128].rearrange("p (n e) -> p n e", e=1024),
                    in_=self.tm['wall'][:, dcol:dcol + CH * 128].rearrange("p (n e) -> p n e", e=1024))
            self.P.op('pool', ldfn, reads=[], writes=[('ring', slot)], dma='ring%d' % slot)

            def wbfn(e, dcol=dcol, slot=slot):
                return e.dma_start(out=self.tm['wbf'][:, dcol:dcol + CH * 128],
                                   in_=self.tm['ring'][:, slot * CH * 128:(slot + 1) * CH * 128])
            if g >= nsc:
                self.P.op('sp', wbfn, reads=[('ring', slot)], writes=[('wbf', g)], dma='wb%d' % slot)
            for k in range(1, lookahead + 1):
                if g + k < nsc + self.nb_real_ch:
                    self._ensure_chunk(g + k, lookahead=0)
        else:
            q = (g - nsc) % nch
            dcol = (self.ns_pad + q * CH) * 128

            def ldfn(e, dcol=dcol, slot=slot):
                return e.dma_start(out=self.tm['ring'][:, slot * CH * 128:(slot + 1) * CH * 128],
                                   in_=self.tm['wbf'][:, dcol:dcol + CH * 128])
            self.P.op('sp', ldfn, reads=[('wbf', nsc + q)], writes=[('ring', slot)], dma='ring%d' % slot)

    def _sc(self, v):
        return v.ap(self.tm) if isinstance(v, R) else v

    def A(self, func, out, in_, scale=None, bias=None):
        reads = [in_.key] + [v.key for v in (scale, bias) if isinstance(v, R)]

        def fn(e):
            kw = {}
            if scale is not None:
                kw['scale'] = self._sc(scale)
            if bias is not None:
                kw['bias'] = self._sc(bias)
            return e.activation(out=out.ap(self.tm), in_=in_.ap(self.tm), func=func, **kw)
        self.P.op('act', fn, reads, [out.key])

    def TT(self, out, in0, in1, op, eng='dve'):
        if eng == 'pool' and not self.pool_ok:
            eng = 'dve'

        def fn(e):
            return e.tensor_tensor(out=out.ap(self.tm), in0=in0.ap(self.tm), in1=in1.ap(self.tm), op=op)
        self.P.op(eng, fn, [in0.key, in1.key], [out.key])

    def TS(self, out, in0, s1, s2, op0, op1=None, eng='dve'):
        reads = [in0.key] + [v.key for v in (s1, s2) if isinstance(v, R)]

        def fn(e):
            if op1 is None:
                return e.tensor_scalar(out=out.ap(self.tm), in0=in0.ap(self.tm), scalar1=self._sc(s1),
                                       scalar2=None, op0=op0)
            return e.tensor_scalar(out=out.ap(self.tm), in0=in0.ap(self.tm), scalar1=self._sc(s1),
                                   scalar2=self._sc(s2), op0=op0, op1=op1)
        self.P.op(eng, fn, reads, [out.key])

    def STT(self, out, in0, scalar, in1, op0, op1):
        reads = [in0.key, in1.key] + ([scalar.key] if isinstance(scalar, R) else [])

        def fn(e):
            return e.scalar_tensor_tensor(out=out.ap(self.tm), in0=in0.ap(self.tm), scalar=self._sc(scalar),
                                          in1=in1.ap(self.tm), op0=op0, op1=op1)
        self.P.op('dve', fn, reads, [out.key])

    def SCAN(self, out, d0, d1, initial):
        def fn(e):
            return e.tensor_tensor_scan(out=out.ap(self.tm), data0=d0.ap(self.tm), data1=d1.ap(self.tm),
                                        initial=initial.ap(self.tm), op0=ALU.mult, op1=ALU.add)
        self.P.op('dve', fn, [d0.key, d1.key, initial.key], [out.key])

    def CP(self, out, in_, eng='dve'):
        if eng == 'pool' and not self.pool_ok:
            return self.A(AF.Copy, out, in_)

        def fn(e):
            return e.tensor_copy(out=out.ap(self.tm), in_=in_.ap(self.tm))
        self.P.op(eng, fn, [in_.key], [out.key])

    def MSET(self, out, val, eng='dve', writes=None):
        def fn(e):
            return e.memset(out.ap(self.tm), val)
        self.P.op(eng, fn, [], writes if writes is not None else [out.key])

    def MM(self, bank, pairs):
        n = len(pairs)
        for i, (l, r) in enumerate(pairs):
            def fn(e, l=l, r=r, i=i):
                return e.matmul(bank.ap(self.tm), lhsT=l.ap(self.tm), rhs=r.ap(self.tm),
                                start=(i == 0), stop=(i == n - 1))
            self.P.op('pe', fn, [l.key, r.key], [bank.key])

    def MM_kouter(self, banks, wspecs):
        for k in range(8):
            for b, (nm, l, c0) in zip(banks, wspecs):
                w = self.W('mat', nm, l, k * 128, c0)
                r = self.xn(k)

                def fn(e, b=b, w=w, r=r, k=k):
                    return e.matmul(b.ap(self.tm), lhsT=w.ap(self.tm), rhs=r.ap(self.tm),
                                    start=(k == 0), stop=(k == 7))
                self.P.op('pe', fn, [w.key, r.key], [b.key])

    def DMA(self, eng, out, in_, chan, reads=None, writes=None):
        def fn(e):
            return e.dma_start(out=out.ap(self.tm), in_=in_.ap(self.tm))
        self.P.op(eng, fn, reads if reads is not None else [in_.key],
                  writes if writes is not None else [out.key], dma=chan)

    def norm_stats(self, srcf, n=512):
        for c in range(8):
            self.A(AF.Square, R('sq', S, c, slice(0, n)), srcf(c).sub(0, n))
        b = self.balloc(n)
        self.MM(b, [(R('onesD', S, S), R('sq', S, c, slice(0, n))) for c in range(8)])
        self.A(AF.Ln, b, b, bias=EPS)
        self.A(AF.Exp, b, b, scale=-0.5)
        return b

    def norm_apply(self, b, srcf, gname, n=512, dst='xn'):
        for c in range(8):
            self.STT(R(dst, S, c, slice(0, n)), srcf(c).sub(0, n), self.Pc(gname, c), b, ALU.mult, ALU.mult)
        self.bfree(b)

    def norm(self, srcf, gname, n=512, dst='xn'):
        b = self.norm_stats(srcf, n)
        self.norm_apply(b, srcf, gname, n, dst)

    def sigmoid_from(self, out, in_, nbias=None):
        self.A(AF.Exp, out, in_, scale=-1.0, bias=nbias)
        self.A(AF.Ln, out, out, bias=1.0)
        self.A(AF.Exp, out, out, scale=-1.0)

    def X(self, c, par=None):
        return R('x', S, self.xp if par is None else par, c, S)

    def resid_add(self, n, bank):
        x = self.X(n)
        self.TT(x, bank, x, ALU.add)

    def xn(self, k):
        return R('xn', S, k, S)

    def setup(self):
        self.phase = 'setup'
        self.DMA('pool', R('P', S, S, key=('P',)), R('params', S, S), 'pl')
        self.DMA('pool', R('f32big', S, S, slice(0, NMEM)), R('memT', S, S, S), 'ml',
                 writes=[('f32big', c) for c in range(8)])
        self.load_x(0)
        self.MSET(R('onesD', S, S), 1.0 / 1024.0)
        self.MSET(R('ones1', S, S), 1.0)
        self.MSET(R('hz', S, S, S), 0.0, writes=[('hz', c) for c in range(8)])
        self.MSET(R('hst', S, S), 0.0)
        self.MSET(self.PDc('lnq', 0, 8), -2.772588722239781)
        self.MSET(R('hf', S, S, S, S), 0.0, writes=[('hf', l, j) for l in range(2) for j in range(24)])
        self.MSET(R('pu', S, S, S), 0.0, writes=[('pu', g) for g in range(4)])
        self.MSET(R('glu', S, S, S), 0.0, writes=[('glu', c) for c in range(8)])
        self.TS(self.PDc('ngxb', 0, 8), self.Pc('a_gate_x_b', 0, 8), -1.0, None, ALU.mult)
        self.TS(self.PDc('ngab', 0, 8), self.Pc('a_gate_a_b', 0, 8), -1.0, None, ALU.mult)
        self.TS(self.PDc('ncb1', 0, 8), self.Pc('c_b_pw1', 8, 8), -1.0, None, ALU.mult)
        s0, s1 = self.PDc('s0', 0, 8), self.PDc('s1', 0, 8)
        lam = self.Pc('a_lambda', 0, 8)
        self.A(AF.Abs, s0, lam)
        self.A(AF.Exp, s0, s0, scale=-1.0)
        self.A(AF.Ln, s0, s0, bias=1.0)
        self.A(AF.Relu, s1, lam, scale=-1.0)
        self.TT(s0, s0, s1, ALU.add)
        self.TS(self.PDc('cvec', 0, 8), s0, -8.0, None, ALU.mult)
        self.TS(self.PDc('c2vec', 0, 8), s0, -16.0, None, ALU.mult)
        for l in range(2):
            self.norm(lambda c: R('f32big', S, c, slice(0, 512)), 'xa_mem_norm%d' % l, n=NMEM, dst='xn')
            for n in range(8):
                b = self.balloc(NMEM)
                self.MM(b, [(self.W('mat', 'xa_wk', l, k * 128, n * 128), R('xn', S, k, slice(0, NMEM)))
                            for k in range(8)])
                self.A(AF.Copy, R('kT', S, l, n, S), b)
                self.bfree(b)
            for dh in range(2):
                b0, b1 = self.balloc(), self.balloc()
                for k in range(8):
                    w = self.W('mat', 'xa_wv', l, k * 128, dh * 512, nblk=4)
                    for bb_, lo in ((b0, 0), (b1, 128)):
                        def fn(e, bb_=bb_, lo=lo, k=k, w=w):
                            return e.matmul(bb_.ap(self.tm), lhsT=self.tm['xn'][:, k, lo:lo + 128], rhs=w.ap(self.tm),
                                            start=(k == 0), stop=(k == 7))
                        self.P.op('pe', fn, [('xn', k), w.key], [bb_.key])
                self.A(AF.Copy, R('v', S, l, 0, slice(dh * 512, dh * 512 + 512)), b0)
                self.A(AF.Copy, R('v', S, l, 1, slice(dh * 512, dh * 512 + 512)), b1)
                self.bfree(b0)
                self.bfree(b1)
        if self.known is None:
            self.ns_pad = -(-len(self.setup_blocks) // CH) * CH

    def load_x(self, t):
        par = t % 2
        self.DMA('pool', R('x', S, par, S, S), R('xT', S, S, slice(t * TT_, (t + 1) * TT_)), 'xl',
                 reads=[], writes=[('x', par, c) for c in range(8)])

    def mix_chunk(self, c):
        br = self.balloc()
        self.MM(br, [(self.W('mat', 'ab_w_in', 0, k * 128, 1024 + c * 128), self.xn(k)) for k in range(8)])
        yield
        bg = self.balloc()
        self.MM(bg, [(self.W('mat', 'ab_w_in', 0, k * 128, c * 128), self.xn(k)) for k in range(8)])
        yield
        acc = self.talloc()
        self.TS(acc, br, self.Pc('a_conv_w3', c), self.Pc('a_conv_b', c), ALU.mult, ALU.add)
        yield
        s_ = self.talloc()
        self.A(AF.Square, s_, bg, scale=0.044715 ** 0.5)
        for s in (1, 2, 3):
            wk = self.Pc('a_conv_w%d' % (3 - s), c)
            self.STT(acc.sub(s, 512), br.sub(0, 512 - s), wk, acc.sub(s, 512), ALU.mult, ALU.add)
            self.STT(acc.sub(0, s), R('hz', S, c, slice(3 - s, 3)), wk, acc.sub(0, s), ALU.mult, ALU.add)
            yield
        self.CP(R('hz', S, c, S), br.sub(509, 512))
        self.bfree(br)
        xrb = self.tballoc()
        self.CP(xrb, acc, eng='pool')
        yield
        self.STT(s_, s_, 1.0, bg, ALU.add, ALU.mult)
        yield
        self.A(AF.Exp, s_, s_, scale=-1.5957691216057308)
        yield
        bx = self.balloc()
        self.MM(bx, [(self.W('blk', 'a_gate_x_w', 0, c), xrb)])
        ba = self.balloc()
        self.MM(ba, [(self.W('blk', 'a_gate_a_w', 0, c), xrb)])
        self.tbfree(xrb)
        self.A(AF.Ln, s_, s_, bias=1.0)
        yield
        self.A(AF.Exp, s_, s_, scale=-1.0)
        yield
        sg = self.talloc()
        self.A(AF.Exp, sg, ba, scale=-1.0, bias=self.PDc('ngab', c))
        self.bfree(ba)
        self.TT(s_, s_, bg, ALU.mult)
        self.bfree(bg)
        yield
        gx = self.talloc()
        self.A(AF.Exp, gx, bx, scale=-1.0, bias=self.PDc('ngxb', c))
        self.bfree(bx)
        yield
        self.A(AF.Ln, sg, sg, bias=1.0)
        yield
        self.A(AF.Ln, gx, gx, bias=1.0)
        yield
        self.A(AF.Exp, sg, sg, scale=-1.0)
        yield
        self.A(AF.Exp, gx, gx, scale=-1.0)
        yield
        a = self.talloc()
        self.A(AF.Exp, a, sg, scale=self.PDc('cvec', c))
        self.tfree(sg)
        yield
        m = self.talloc()
        self.TT(m, a, a, ALU.mult, eng='pool')
        self.TT(gx, gx, acc, ALU.mult, eng='pool')
        self.tfree(acc)
        yield
        self.A(AF.Ln, m, m, scale=-(1.0 - 1e-6), bias=1.0)
        yield
        self.A(AF.Exp, m, m, scale=0.5)
        yield
        self.TT(gx, gx, m, ALU.mult)
        self.tfree(m)
        yield
        h = self.talloc()
        self.SCAN(h, a, gx, R('hst', S, slice(c, c + 1)))
        self.tfree(a)
        self.tfree(gx)
        yield
        self.CP(R('hst', S, slice(c, c + 1)), h.sub(511, 512))
        self.TT(R('big', S, c, S), s_, h, ALU.mult)
        self.tfree(h)
        self.tfree(s_)
        yield

    def pool_group(self, t, g):
        w = 2 ** (g + 1)
        bp = self.balloc()
        self.MM(bp, [(self.W('mat', 'ab_w_in', 0, k * 128, 2048 + g * 128), self.xn(k)) for k in range(8)])
        yield
        u = R('pu', S, g, slice(16, 528))
        self.A(AF.Copy, u, bp)
        self.bfree(bp)
        yield
        src = ('pu', g)
        m_ = 1
        for i in range(g + 1):
            dst = 'psA%d' % (g % 2) if i % 2 == 0 else 'psB%d' % (g % 2)
            lo = 2 * m_

            def mk(nm, a_, b_):
                return R('pu', S, g, slice(a_, b_)) if nm[0] == 'pu' else R(nm[0], S, slice(a_, b_))
            self.TT(R(dst, S, slice(lo, 528)), mk(src, lo, 528), mk(src, lo - m_, 528 - m_), ALU.add, eng='pool')
            src = (dst,)
            m_ *= 2
            yield
        for _ in range(8):
            yield
        sw = R(src[0], S, slice(16, 528))
        pb = self.tballoc()
        self.STT(pb, sw, 1.0 / w, u, ALU.mult, ALU.subtract)
        if t == 0 and w > 1:
            tc_ = self.talloc()
            self.TT(tc_.sub(0, w - 1), sw.sub(0, w - 1), self.Pc('pf', g * 16, w - 1), ALU.mult)
            self.TT(pb.sub(0, w - 1), tc_.sub(0, w - 1), u.sub(0, w - 1), ALU.subtract)
            self.tfree(tc_)
        self.CP(R('pu', S, g, slice(1, 16)), R('pu', S, g, slice(513, 528)))
        self.pb[g] = pb
        yield

    def pool_group_b(self, g):
        pb = self.pb.pop(g)
        bq = self.balloc()
        self.MM(bq, [(self.W('blk', 'b_group_w', 0, g), pb)])
        self.tbfree(pb)
        self.TS(R('big', S, 8 + g, S), bq, self.Pc('b_group_b', g), self.Pc('b_scale', g), ALU.add, ALU.mult)
        self.bfree(bq)

    def mixer(self, t, prenormed=False):
        if not prenormed:
            self.norm(self.X, 'ab_norm')
        extra = [R('f32big', S, c, slice(0, 512)) for c in range(8)]
        self.tfree_.extend(extra)
        ch = [self.mix_chunk(c) for c in range(8)]
        pg = [self.pool_group(t, g) for g in range(4)]
        gens = [ch[0], ch[1], pg[3], ch[2], pg[2], ch[3], pg[1], ch[4], pg[0], ch[5], ch[6], ch[7]]
        self.pb = {}
        chunks = [g for g in gens if g.gi_code.co_name == 'mix_chunk']
        ka = [0, 1, 2, 3, 4, 8, 9, 10, 11]
        kb = [5, 6, 7]

        def wout_pass(ks, first):
            if first:
                while len(self.pb) < 4:
                    yield
                for g in (3, 2, 1, 0):
                    self.pool_group_b(g)
                    yield
            for n in range(8):
                b = self.balloc()
                self.MM(b, [(self.W('mat', 'ab_w_out', 0, k * 128, n * 128), R('big', S, k, S)) for k in ks])
                self.resid_add(n, b)
                self.bfree(b)
                yield
        self.pipeline(gens, MIX_STAGGER, triggers={id(chunks[4]): [wout_pass(ka, True)]})
        self.tfree_ = [r for r in self.tfree_ if r.name != 'f32big']
        assert len(self.tfree_) == NTMP and len(self.bfree_) == 8 and len(self.tbfree_) == NTB
        for _ in wout_pass(kb, False):
            pass

    def xattn(self, l):
        for c in range(8):
            self.A(AF.Square, R('sq', S, c, S), self.X(c))
            self.A(AF.Identity, R('xn', S, c, S), self.X(c), scale=self.Pc('xa_norm%d' % l, c))
        bst = self.balloc()
        self.MM(bst, [(R('onesD', S, S), R('sq', S, c, S)) for c in range(8)])
        rs = self.talloc()
        self.A(AF.Ln, rs, bst, bias=EPS)
        self.bfree(bst)
        self.A(AF.Exp, rs, rs, scale=-0.5, bias=self.PDc('lnq', 0))
        for n in range(8):
            b = self.balloc()
            self.MM(b, [(self.W('mat', 'xa_wq', l, k * 128, n * 128), self.xn(k)) for k in range(8)])
            self.TT(R('big', S, n, S), b, rs, ALU.mult)
            self.bfree(b)
        self.tfree(rs)
        for hh in range(4):
            ex = []
            for mc in range(2):
                b = self.balloc()
                self.MM(b, [(R('kT', S, l, 2 * hh + dc, slice(mc * 128, (mc + 1) * 128)),
                             R('big', S, 2 * hh + dc, S)) for dc in range(2)])
                e = self.tballoc()
                self.A(AF.Exp, e, b)
                self.bfree(b)
                ex.append(e)
            bd = self.balloc()
            self.MM(bd, [(R('ones1', S, S), ex[mc]) for mc in range(2)])
            rd = self.talloc()
            self.A(AF.Ln, rd, bd)
            self.bfree(bd)
            self.A(AF.Exp, rd, rd, scale=-1.0)
            for dc in range(2):
                bo = self.balloc()
                d0 = (2 * hh + dc) * 128
                self.MM(bo, [(R('v', S, l, mc, slice(d0, d0 + 128)), ex[mc]) for mc in range(2)])
                self.TT(R('big', S, 8 + 2 * hh + dc, S), bo, rd, ALU.mult)
                self.bfree(bo)
            self.tfree(rd)
            for e in ex:
                self.tbfree(e)
        for n in range(8):
            b = self.balloc()
            self.MM(b, [(self.W('mat', 'xa_wo', l, k * 128, n * 128), R('big', S, 8 + k, S)) for k in range(8)])
            self.resid_add(n, b)
            self.bfree(b)

    def ffn_chunk(self, l, j):
        if j in self.ffn_pre:
            bg, bu = self.ffn_pre.pop(j)
        else:
            bg = self.balloc()
            self.MM(bg, [(self.W('mat', 'f_w_up', l, k * 128, j * 128), self.xn(k)) for k in range(8)])
            bu = self.balloc()
            self.MM(bu, [(self.W('mat', 'f_w_up', l, k * 128, 3072 + j * 128), self.xn(k)) for k in range(8)])
        yield
        acc = self.talloc()
        self.A(AF.Identity, acc, bg, scale=self.Pc('f_dw_w%d_2' % l, j), bias=self.Pc('f_dw_b%d' % l, j))
        yield
        for s in (1, 2):
            wk = self.Pc('f_dw_w%d_%d' % (l, 2 - s), j)
            self.STT(acc.sub(s, 512), bg.sub(0, 512 - s), wk, acc.sub(s, 512), ALU.mult, ALU.add)
            self.STT(acc.sub(0, s), R('hf', S, l, j, slice(2 - s, 2)), wk, acc.sub(0, s), ALU.mult, ALU.add)
        self.CP(R('hf', S, l, j, S), bg.sub(510, 512))
        self.bfree(bg)
        yield
        self.A(AF.Gelu_apprx_tanh, acc, acc)
        yield
        self.TT(R('big', S, j, S), acc, bu, ALU.mult)
        self.tfree(acc)
        self.bfree(bu)
        yield

    def ffn(self, l, hooks=None):
        self.norm(self.X, 'f_norm%d' % l)
        pre = [self.balloc() for _ in range(4)]
        self.MM_kouter(pre, [('f_w_up', l, 0), ('f_w_up', l, 3072), ('f_w_up', l, 128), ('f_w_up', l, 3072 + 128)])
        self.ffn_pre = {0: (pre[0], pre[1]), 1: (pre[2], pre[3])}
        self.pipeline([self.ffn_chunk(l, j) for j in range(24)], 1)
        for n in range(8):
            if hooks is not None and n in hooks:
                hooks[n]()
            b = self.balloc()
            self.MM(b, [(self.W('mat', 'f_w_down', l, k * 128, n * 128), R('big', S, k, S)) for k in range(24)])
            self.resid_add(n, b)
            self.bfree(b)

    def conformer(self):
        self.norm(self.X, 'c_norm')
        pre = [self.balloc() for _ in range(4)]
        self.MM_kouter(pre, [('c_w_pw1', 0, 0), ('c_w_pw1', 0, 1024), ('c_w_pw1', 0, 128), ('c_w_pw1', 0, 1024 + 128)])
        cpre = {0: (pre[0], pre[1]), 1: (pre[2], pre[3])}
        for c in range(8):
            if c in cpre:
                ba, bb = cpre.pop(c)
            else:
                ba = self.balloc()
                self.MM(ba, [(self.W('mat', 'c_w_pw1', 0, k * 128, c * 128), self.xn(k)) for k in range(8)])
                bb = self.balloc()
                self.MM(bb, [(self.W('mat', 'c_w_pw1', 0, k * 128, 1024 + c * 128), self.xn(k)) for k in range(8)])
            sg = self.talloc()
            self.sigmoid_from(sg, bb, self.PDc('ncb1', c))
            self.bfree(bb)
            self.STT(R('glu', S, c, slice(30, 542)), ba, self.Pc('c_b_pw1', c), sg, ALU.add, ALU.mult)
            self.tfree(sg)
            self.bfree(ba)
        for c in range(8):
            bc = self.balloc()
            self.MM(bc, [(self.W('diag', 'c_dw_w', 0, k, c), R('glu', S, c, slice(k, k + 512))) for k in range(31)])
            bia = self.Pc('c_dw_b', c)
            self.A(AF.Identity, R('f32big', S, c, S), bc, bias=bia)
            self.A(AF.Identity, R('xn', S, c, S), bc, bias=bia)
            self.A(AF.Square, R('sq', S, c, S), bc, bias=bia)
            self.bfree(bc)
            self.CP(R('glu', S, c, slice(0, 30)), R('glu', S, c, slice(512, 542)), eng='pool')
        bm = self.balloc()
        self.MM(bm, [(R('onesD', S, S), R('xn', S, c, S)) for c in range(8)])
        bq = self.balloc()
        self.MM(bq, [(R('onesD', S, S), R('sq', S, c, S)) for c in range(8)])
        msq = self.talloc()
        self.A(AF.Square, msq, bm)
        self.TT(bq, bq, msq, ALU.subtract)
        self.tfree(msq)
        self.A(AF.Ln, bq, bq, bias=EPS)
        self.A(AF.Exp, bq, bq, scale=-0.5)
        def ln_chunk(c):
            t2 = self.talloc()
            self.TT(t2, R('f32big', S, c, S), bm, ALU.subtract)
            yield
            self.TT(t2, t2, bq, ALU.mult)
            yield
            self.A(AF.Silu, R('big', S, c, S), t2, scale=self.Pc('c_ln_g', c), bias=self.Pc('c_ln_b', c))
            self.tfree(t2)
            yield
        self.pipeline([ln_chunk(c) for c in range(8)], 1)
        self.bfree(bm)
        self.bfree(bq)
        for n in range(8):
            b = self.balloc()
            self.MM(b, [(self.W('mat', 'c_w_pw2', 0, k * 128, n * 128), R('big', S, k, S)) for k in range(8)])
            x = self.X(n)
            self.STT(x, b, self.Pc('c_b_pw2', n), x, ALU.add, ALU.add)
            self.bfree(b)

    def trace(self):
        self.xp = 0
        self.setup()
        self.phase = 'tile'
        st = self.stages
        prenormed = False
        for t in range(self.nt):
            self.tile = t
            self.bpos = 0
            self.xp = t % 2
            self.pool_ok = t > 0
            more = t + 1 < self.nt
            if 'mix0' in st:
                self.mixer(t, prenormed)
            if more:
                self.load_x(t + 1)
            if 'xa0' in st:
                self.xattn(0)
            if 'ffn0' in st:
                self.ffn(0)
            if 'conf1' in st:
                self.conformer()
            if 'xa1' in st:
                self.xattn(1)
            prenormed = False
            if 'ffn1' in st:
                hooks = None
                if more and 'mix0' in st:
                    nx = (lambda c, par=(t + 1) % 2: self.X(c, par))
                    hold = {}
                    hooks = {1: (lambda: hold.__setitem__('b', self.norm_stats(nx))),
                             4: (lambda: self.norm_apply(hold['b'], nx, 'ab_norm'))}
                    prenormed = True
                self.ffn(1, hooks)
            if 'final' in st:
                self.norm(self.X, 'final_norm', dst='f32big')
                src = [R('f32big', S, c, S) for c in range(8)]
            else:
                src = [self.X(c) for c in range(8)]
            par = self.xp
            if more and self.known is not None:
                for q in range(NSLOT - 1):
                    self._ensure_chunk(self.ns_pad // CH + (t + 1) * (self.nb_pad // CH) + q, lookahead=0)

            def fn(e, t=t, par=par, final=('final' in st)):
                in_ = self.tm['f32big'][:, :, :] if final else self.tm['x'][:, par, :, :]
                return e.dma_start(out=self.tm['outT'][:, :, t * TT_:(t + 1) * TT_], in_=in_)
            self.P.op('sp', fn, [r.key for r in src], [('outT',)], dma='os')
            if t == 0 and self.known is None:
                self.nb_pad = -(-len(self.tile_blocks) // CH) * CH
        if self.known is None:
            self.ncols = (self.ns_pad + self.nb_pad) * 128

    def emit(self):
        nc = bass.Bass("TRN2", target_bir_lowering=False)
        tm = {}
        self.tm = tm
        seq = self.nt * TT_
        tm['xT'] = nc.dram_tensor("xT", [128, 8, seq], F32, kind="ExternalInput").ap()
        tm['memT'] = nc.dram_tensor("memT", [128, 8, NMEM], F32, kind="ExternalInput").ap()
        tm['wall'] = nc.dram_tensor("wall", [128, self.ncols], F32, kind="ExternalInput").ap()
        tm['params'] = nc.dram_tensor("params", [128, self.np_cols], F32, kind="ExternalInput").ap()
        tm['wbf'] = nc.dram_tensor("wbf", [128, self.ncols], BF16, kind="Internal").ap()
        tm['outT'] = nc.dram_tensor("outT", [128, 8, seq], F32, kind="ExternalOutput").ap()
        sb = [
            ('x', [128, 2, 8, 512], F32), ('f32big', [128, 8, 512], F32),
            ('sq', [128, 8, 512], BF16), ('xn', [128, 8, 512], BF16),
            ('big', [128, 24, 512], BF16), ('glu', [128, 8, 542], BF16),
            ('ring', [128, NSLOT * CH * 128], BF16),
            ('pu', [128, 4, 528], F32), ('psA0', [128, 528], F32), ('psB0', [128, 528], F32),
            ('psA1', [128, 528], F32), ('psB1', [128, 528], F32),
            ('kT', [128, 2, 8, NMEM], BF16), ('v', [128, 2, 2, 1024], BF16),
            ('P', [128, self.np_cols], F32), ('PD', [128, self.npd_cols], F32),
            ('hz', [128, 8, 3], F32), ('hst', [128, 8], F32), ('hf', [128, 2, 24, 2], F32),
            ('onesD', [128, 128], BF16), ('ones1', [128, 128], BF16),
        ]
        sb += [('tmp%d' % i, [128, 512], F32) for i in range(NTMP)]
        sb += [('tb%d' % i, [128, 512], BF16) for i in range(NTB)]
        with ExitStack() as es:
            for name, shape, dt in sb:
                tm[name] = es.enter_context(nc.sbuf_tensor(name, shape, dt))
            for i in range(8):
                tm['ps%d' % i] = es.enter_context(nc.psum_tensor('ps%d' % i, [128, 512], F32))
            sems = {n: es.enter_context(nc.semaphore(n)) for n in self.P.sem_names}
            block = es.enter_context(nc.Block())
            last_os = None
            for o in self.P.ops:
                if o.dma == 'os':
                    last_os = o.comp

            @block.tensor
            def _(e):
                self.P.emit_engine('pe', e, sems)

            @block.scalar
            def _(e):
                self.P.emit_engine('act', e, sems)

            @block.vector
            def _(e):
                self.P.emit_engine('dve', e, sems)

            @block.gpsimd
            def _(e):
                self.P.emit_engine('pool', e, sems)

            @block.sync
            def _(e):
                self.P.emit_engine('sp', e, sems)
                e.wait_ge(sems[last_os[0]], last_os[1])
        return nc

    def host_wall(self, inp):
        wall = np.zeros((128, self.ncols), np.float32)

        def fill(col, spec):
            kind = spec[0]
            if kind == 'mat':
                _, name, l, r0, c0 = spec[:5]
                j = spec[5] if len(spec) > 5 else 0
                blk = inp[name][l][r0:r0 + 128, c0 + j * 128:c0 + (j + 1) * 128]
            elif kind == 'blk':
                _, name, l, h = spec
                blk = inp[name][l][h]
            elif kind == 'diag':
                _, name, l, k, c = spec
                blk = np.diag(inp[name][l][k, c * 128:(c + 1) * 128])
            wall[:, col:col + 128] = blk
        for i, s in enumerate(self.setup_blocks):
            fill(i * 128, s)
        for i, s in enumerate(self.tile_blocks):
            fill((self.ns_pad + i) * 128, s)
        return wall

    def host_params(self, inp):
        Pm = np.zeros((128, self.np_cols), np.float32)

        def put(name, vec):
            vec = np.asarray(vec, np.float32).reshape(-1, 128)
            c = self.pcols[name]
            Pm[:, c:c + vec.shape[0]] = vec.T
        put('ab_norm', inp['ab_norm'][0])
        for k in range(4):
            put('a_conv_w%d' % k, inp['a_conv_w'][0][k])
        for n in ('a_conv_b', 'a_gate_x_b', 'a_gate_a_b', 'a_lambda', 'b_group_b', 'b_scale',
                  'c_norm', 'c_b_pw1', 'c_dw_b', 'c_ln_g', 'c_ln_b', 'c_b_pw2'):
            put(n, inp[n][0])
        for l in range(2):
            put('xa_norm%d' % l, inp['xa_norm'][l])
            put('xa_mem_norm%d' % l, inp['xa_mem_norm'][l])
            put('f_norm%d' % l, inp['f_norm'][l])
            for k in range(3):
                put('f_dw_w%d_%d' % (l, k), inp['f_dw_w'][l][k])
            put('f_dw_b%d' % l, inp['f_dw_b'][l])
        put('final_norm', inp['final_norm'])
        c = self.pcols['pf']
        for g in range(4):
            w = 2 ** (g + 1)
            for t in range(16):
                Pm[:, c + g * 16 + t] = 1.0 / min(t + 1, w)
        return Pm


_CACHE = {}


def get_builder(nt=SEQ // TT_, stages=ALL_STAGES):
    key = (nt, tuple(stages))
    if key not in _CACHE:
        dry = Builder(1, stages)
        _CACHE[key] = Builder(nt, stages, known=(dry.setup_blocks, dry.tile_blocks))
    return _CACHE[key]


def run(inp, nt=SEQ // TT_, stages=ALL_STAGES, ncores=8, trace=False):
    B = get_builder(nt, stages)
    inp = {k: np.asarray(v) for k, v in inp.items()}
    nc = B.emit()
    wall = B.host_wall(inp)
    params = B.host_params(inp)
    seq = nt * TT_
    in_maps = []
    for i in range(ncores):
        xT = np.ascontiguousarray(inp['x'][i, :seq].T.reshape(8, 128, seq).transpose(1, 0, 2))
        memT = np.ascontiguousarray(inp['mem'][i].T.reshape(8, 128, NMEM).transpose(1, 0, 2))
        in_maps.append({'xT': xT, 'memT': memT, 'wall': wall, 'params': params})
    res = run_bass_kernel_spmd(nc, in_maps, core_ids=list(range(ncores)), trace=trace)
    outs = []
    for i in range(ncores):
        o = np.asarray(res.results[i]['outT'])
        outs.append(o.transpose(1, 0, 2).reshape(1024, seq).T)
    return np.stack(outs).astype(np.float32), res


def kernel(**inputs):
    out, _ = run(inputs)
    return out
```

```python
import numpy as np
from contextlib import ExitStack
import concourse.bass as bass
import concourse.mybir as mybir
from concourse.bass_utils import run_bass_kernel_spmd

F32, BF16 = mybir.dt.float32, mybir.dt.bfloat16
AF = mybir.ActivationFunctionType
ALU = mybir.AluOpType
S = slice(None)
TT_ = 512
SEQ = 4096
NMEM = 256
CH = 32
NSLOT = 4
PIECE = 4
EPS = 1e-6
NTMP = 12
NTB = 8
MIX_STAGGER = 6
ALL_STAGES = ('mix0', 'xa0', 'ffn0', 'conf1', 'xa1', 'ffn1', 'final')


class R:
    __slots__ = ('name', 'idx', 'key')

    def __init__(self, name, *idx, key=None):
        self.name = name
        self.idx = idx
        self.key = key if key is not None else (name,) + tuple(i for i in idx if isinstance(i, int))

    def ap(self, tm):
        t = tm[self.name]
        return t[self.idx] if self.idx else t[:]

    def sub(self, a, b):
        idx = list(self.idx)
        last = idx[-1]
        start = 0 if last.start is None else last.start
        idx[-1] = slice(start + a, start + b)
        return R(self.name, *idx, key=self.key)


class Op:
    __slots__ = ('eng', 'fn', 'deps', 'dma', 'sig', 'comp')

    def __init__(self, eng, fn, deps, dma):
        self.eng, self.fn, self.deps, self.dma = eng, fn, deps, dma
        self.sig = False
        self.comp = None


class Prog:
    def __init__(self):
        self.ops = []
        self.last_w = {}
        self.readers = {}

    def op(self, eng, fn, reads=(), writes=(), dma=None):
        idx = len(self.ops)
        deps = {}
        for k in reads:
            w = self.last_w.get(k)
            if w is not None:
                deps[w] = 'RAW'
        for k in writes:
            w = self.last_w.get(k)
            if w is not None:
                deps.setdefault(w, 'WAW')
            for r in self.readers.get(k, {}).values():
                deps.setdefault(r, 'WAR')
        chan = dma if dma is not None else eng
        for k in reads:
            self.readers.setdefault(k, {})[chan] = idx
        for k in writes:
            self.last_w[k] = idx
            self.readers[k] = {}
        self.ops.append(Op(eng, fn, deps, dma))
        return idx

    def finalize(self):
        ops = self.ops
        for o in ops:
            for d, typ in o.deps.items():
                p = ops[d]
                if p.dma is not None:
                    continue
                if o.dma is not None or p.eng != o.eng or typ == 'RAW':
                    p.sig = True
        cnt = {}
        for o in ops:
            if o.dma is not None:
                cnt[o.dma] = cnt.get(o.dma, 0) + 16
                o.comp = ('dma_' + o.dma, cnt[o.dma])
            elif o.sig:
                cnt[o.eng] = cnt.get(o.eng, 0) + 1
                o.comp = ('eng_' + o.eng, cnt[o.eng])
        self.sem_names = sorted(set(o.comp[0] for o in ops if o.comp is not None))

    def emit_engine(self, engname, eng, sems):
        ops = self.ops
        waited = {}
        for o in ops:
            if o.eng != engname:
                continue
            for d, typ in o.deps.items():
                p = ops[d]
                if p.dma is None and not (o.dma is not None or p.eng != o.eng or typ == 'RAW'):
                    continue
                sname, val = p.comp
                if waited.get(sname, 0) >= val:
                    continue
                waited[sname] = val
                eng.wait_ge(sems[sname], val)
            ins = o.fn(eng)
            if o.comp is not None:
                ins.then_inc(sems[o.comp[0]], 16 if o.dma is not None else 1)
        return waited


class Builder:
    def __init__(self, nt=SEQ // TT_, stages=ALL_STAGES, known=None):
        self.nt = nt
        self.stages = tuple(stages)
        self.P = Prog()
        self.setup_blocks = []
        self.tile_blocks = []
        self.known = known
        self.pool_ok = False
        if known is not None:
            self.ns_pad = -(-len(known[0]) // CH) * CH
            self.nb_pad = -(-len(known[1]) // CH) * CH
            self.nb_real_ch = -(-len(known[1]) // CH)
            self.ncols = (self.ns_pad + self.nb_pad) * 128
        else:
            self.nb_pad = None
            self.ns_pad = None
        self.chunks_issued = set()
        self.pieces_cast = set()
        self.phase = 'setup'
        self.tile = -1
        self.bpos = 0
        self.bfree_ = [R('ps%d' % i, S, slice(0, 512)) for i in range(8)]
        self.tfree_ = [R('tmp%d' % i, S, slice(0, 512)) for i in range(NTMP)]
        self.tbfree_ = [R('tb%d' % i, S, slice(0, 512)) for i in range(NTB)]
        self.pcols = {}
        self.np_cols = 0
        self.pdcols = {}
        self.npd_cols = 0
        self._param_layout()
        self.trace()
        self.P.finalize()

    def _padd(self, name, n):
        self.pcols[name] = self.np_cols
        self.np_cols += n

    def _pdadd(self, name, n):
        self.pdcols[name] = self.npd_cols
        self.npd_cols += n

    def _param_layout(self):
        a = self._padd
        a('ab_norm', 8)
        for k in range(4):
            a('a_conv_w%d' % k, 8)
        for n in ('a_conv_b', 'a_gate_x_b', 'a_gate_a_b', 'a_lambda'):
            a(n, 8)
        a('b_group_b', 4)
        a('b_scale', 4)
        a('c_norm', 8)
        a('c_b_pw1', 16)
        for n in ('c_dw_b', 'c_ln_g', 'c_ln_b', 'c_b_pw2'):
            a(n, 8)
        for l in range(2):
            a('xa_norm%d' % l, 8)
            a('xa_mem_norm%d' % l, 8)
            a('f_norm%d' % l, 8)
            for k in range(3):
                a('f_dw_w%d_%d' % (l, k), 24)
            a('f_dw_b%d' % l, 24)
        a('final_norm', 8)
        a('pf', 64)
        for n in ('ngxb', 'ngab', 'ncb1', 'cvec', 'c2vec', 's0', 's1', 'lnq'):
            self._pdadd(n, 8)

    def Pc(self, name, i=0, n=1):
        c = self.pcols[name] + i
        return R('P', S, slice(c, c + n), key=('P',))

    def PDc(self, name, i=0, n=1):
        c = self.pdcols[name] + i
        return R('PD', S, slice(c, c + n), key=('PD', name))

    def balloc(self, n=512):
        assert self.bfree_, 'PSUM banks exhausted'
        b = self.bfree_.pop(0)
        return b if n == 512 else b.sub(0, n)

    def bfree(self, b):
        self.bfree_.append(R(b.name, S, slice(0, 512)))

    def talloc(self, n=512):
        assert self.tfree_, 'tmp pool exhausted'
        t = self.tfree_.pop(0)
        return t if n == 512 else t.sub(0, n)

    def tfree(self, t):
        idx = list(t.idx)
        idx[-1] = slice(0, 512)
        self.tfree_.append(R(t.name, *idx))

    def tballoc(self):
        assert self.tbfree_, 'bf16 tmp pool exhausted'
        return self.tbfree_.pop(0)

    def tbfree(self, t):
        self.tbfree_.append(R(t.name, S, slice(0, 512)))

    def pipeline(self, gens, stagger, triggers=None):
        active, nxt, rnd = [], 0, 0
        triggers = triggers or {}
        while nxt < len(gens) or active:
            if nxt < len(gens) and rnd >= nxt * stagger:
                active.append(gens[nxt])
                nxt += 1
            for g in list(active):
                try:
                    next(g)
                except StopIteration:
                    active.remove(g)
                    active.extend(triggers.get(id(g), []))
            rnd += 1

    def W(self, *spec, nblk=1):
        if self.phase == 'setup':
            pos = len(self.setup_blocks)
            for j in range(nblk):
                self.setup_blocks.append(spec + ((j,) if nblk > 1 else ()))
            gb = pos
        else:
            pos = self.bpos
            for j in range(nblk):
                s = spec + ((j,) if nblk > 1 else ())
                if self.tile == 0:
                    self.tile_blocks.append(s)
                else:
                    assert self.tile_blocks[pos + j] == s
            self.bpos += nblk
            nbp = self.nb_pad if (self.tile > 0) else 0
            gb = self.ns_pad + self.tile * nbp + pos
        g0, g1 = gb // CH, (gb + nblk - 1) // CH
        assert g0 == g1
        self._ensure_chunk(g0)
        slot = g0 % NSLOT
        off = (slot * CH + gb % CH) * 128
        return R('ring', S, slice(off, off + 128 * nblk), key=('ring', slot))

    def _ensure_chunk(self, g, lookahead=2):
        if g in self.chunks_issued:
            return
        self.chunks_issued.add(g)
        slot = g % NSLOT
        if self.known is None:
            self.P.op('sp', lambda e: None, reads=[], writes=[('ring', slot)], dma='ring%d' % slot)
            return
        nsc = self.ns_pad // CH
        nch = self.nb_pad // CH
        first_pass = g < nsc + nch
        if first_pass:
            dcol = g * CH * 128

            def ldfn(e, dcol=dcol, slot=slot):
                return e.dma_start(
                    out=self.tm['ring'][:, slot * CH * 128:(slot + 1) * CH * 128].rearrange("p (n e) -> p n e", e=1024),
                    in_=self.tm['wall'][:, dcol:dcol + CH * 128].rearrange("p (n e) -> p n e", e=1024))
            self.P.op('pool', ldfn, reads=[], writes=[('ring', slot)], dma='ring%d' % slot)

            def wbfn(e, dcol=dcol, slot=slot):
                return e.dma_start(out=self.tm['wbf'][:, dcol:dcol + CH * 128],
                                   in_=self.tm['ring'][:, slot * CH * 128:(slot + 1) * CH * 128])
            if g >= nsc:
                self.P.op('sp', wbfn, reads=[('ring', slot)], writes=[('wbf', g)], dma='wb%d' % slot)
            for k in range(1, lookahead + 1):
                if g + k < nsc + self.nb_real_ch:
                    self._ensure_chunk(g + k, lookahead=0)
        else:
            q = (g - nsc) % nch
            dcol = (self.ns_pad + q * CH) * 128

            def ldfn(e, dcol=dcol, slot=slot):
                return e.dma_start(out=self.tm['ring'][:, slot * CH * 128:(slot + 1) * CH * 128],
                                   in_=self.tm['wbf'][:, dcol:dcol + CH * 128])
            self.P.op('sp', ldfn, reads=[('wbf', nsc + q)], writes=[('ring', slot)], dma='ring%d' % slot)

    def _sc(self, v):
        return v.ap(self.tm) if isinstance(v, R) else v

    def A(self, func, out, in_, scale=None, bias=None):
        reads = [in_.key] + [v.key for v in (scale, bias) if isinstance(v, R)]

        def fn(e):
            kw = {}
            if scale is not None:
                kw['scale'] = self._sc(scale)
            if bias is not None:
                kw['bias'] = self._sc(bias)
            return e.activation(out=out.ap(self.tm), in_=in_.ap(self.tm), func=func, **kw)
        self.P.op('act', fn, reads, [out.key])

    def TT(self, out, in0, in1, op, eng='dve'):
        if eng == 'pool' and not self.pool_ok:
            eng = 'dve'

        def fn(e):
            return e.tensor_tensor(out=out.ap(self.tm), in0=in0.ap(self.tm), in1=in1.ap(self.tm), op=op)
        self.P.op(eng, fn, [in0.key, in1.key], [out.key])

    def TS(self, out, in0, s1, s2, op0, op1=None, eng='dve'):
        reads = [in0.key] + [v.key for v in (s1, s2) if isinstance(v, R)]

        def fn(e):
            if op1 is None:
                return e.tensor_scalar(out=out.ap(self.tm), in0=in0.ap(self.tm), scalar1=self._sc(s1),
                                       scalar2=None, op0=op0)
            return e.tensor_scalar(out=out.ap(self.tm), in0=in0.ap(self.tm), scalar1=self._sc(s1),
                                   scalar2=self._sc(s2), op0=op0, op1=op1)
        self.P.op(eng, fn, reads, [out.key])

    def STT(self, out, in0, scalar, in1, op0, op1):
        reads = [in0.key, in1.key] + ([scalar.key] if isinstance(scalar, R) else [])

        def fn(e):
            return e.scalar_tensor_tensor(out=out.ap(self.tm), in0=in0.ap(self.tm), scalar=self._sc(scalar),
                                          in1=in1.ap(self.tm), op0=op0, op1=op1)
        self.P.op('dve', fn, reads, [out.key])

    def SCAN(self, out, d0, d1, initial):
        def fn(e):
            return e.tensor_tensor_scan(out=out.ap(self.tm), data0=d0.ap(self.tm), data1=d1.ap(self.tm),
                                        initial=initial.ap(self.tm), op0=ALU.mult, op1=ALU.add)
        self.P.op('dve', fn, [d0.key, d1.key, initial.key], [out.key])

    def CP(self, out, in_, eng='dve'):
        if eng == 'pool' and not self.pool_ok:
            return self.A(AF.Copy, out, in_)

        def fn(e):
            return e.tensor_copy(out=out.ap(self.tm), in_=in_.ap(self.tm))
        self.P.op(eng, fn, [in_.key], [out.key])

    def MSET(self, out, val, eng='dve', writes=None):
        def fn(e):
            return e.memset(out.ap(self.tm), val)
        self.P.op(eng, fn, [], writes if writes is not None else [out.key])

    def MM(self, bank, pairs):
        n = len(pairs)
        for i, (l, r) in enumerate(pairs):
            def fn(e, l=l, r=r, i=i):
                return e.matmul(bank.ap(self.tm), lhsT=l.ap(self.tm), rhs=r.ap(self.tm),
                                start=(i == 0), stop=(i == n - 1))
            self.P.op('pe', fn, [l.key, r.key], [bank.key])

    def MM_kouter(self, banks, wspecs):
        for k in range(8):
            for b, (nm, l, c0) in zip(banks, wspecs):
                w = self.W('mat', nm, l, k * 128, c0)
                r = self.xn(k)

                def fn(e, b=b, w=w, r=r, k=k):
                    return e.matmul(b.ap(self.tm), lhsT=w.ap(self.tm), rhs=r.ap(self.tm),
                                    start=(k == 0), stop=(k == 7))
                self.P.op('pe', fn, [w.key, r.key], [b.key])

    def DMA(self, eng, out, in_, chan, reads=None, writes=None):
        def fn(e):
            return e.dma_start(out=out.ap(self.tm), in_=in_.ap(self.tm))
        self.P.op(eng, fn, reads if reads is not None else [in_.key],
                  writes if writes is not None else [out.key], dma=chan)

    def norm_stats(self, srcf, n=512):
        for c in range(8):
            self.A(AF.Square, R('sq', S, c, slice(0, n)), srcf(c).sub(0, n))
        b = self.balloc(n)
        self.MM(b, [(R('onesD', S, S), R('sq', S, c, slice(0, n))) for c in range(8)])
        self.A(AF.Ln, b, b, bias=EPS)
        self.A(AF.Exp, b, b, scale=-0.5)
        return b

    def norm_apply(self, b, srcf, gname, n=512, dst='xn'):
        for c in range(8):
            self.STT(R(dst, S, c, slice(0, n)), srcf(c).sub(0, n), self.Pc(gname, c), b, ALU.mult, ALU.mult)
        self.bfree(b)

    def norm(self, srcf, gname, n=512, dst='xn'):
        b = self.norm_stats(srcf, n)
        self.norm_apply(b, srcf, gname, n, dst)

    def sigmoid_from(self, out, in_, nbias=None):
        self.A(AF.Exp, out, in_, scale=-1.0, bias=nbias)
        self.A(AF.Ln, out, out, bias=1.0)
        self.A(AF.Exp, out, out, scale=-1.0)

    def X(self, c, par=None):
        return R('x', S, self.xp if par is None else par, c, S)

    def resid_add(self, n, bank):
        x = self.X(n)
        self.TT(x, bank, x, ALU.add)

    def xn(self, k):
        return R('xn', S, k, S)

    def setup(self):
        self.phase = 'setup'
        self.DMA('pool', R('P', S, S, key=('P',)), R('params', S, S), 'pl')
        self.DMA('pool', R('f32big', S, S, slice(0, NMEM)), R('memT', S, S, S), 'ml',
                 writes=[('f32big', c) for c in range(8)])
        self.load_x(0)
        self.MSET(R('onesD', S, S), 1.0 / 1024.0)
        self.MSET(R('ones1', S, S), 1.0)
        self.MSET(R('hz', S, S, S), 0.0, writes=[('hz', c) for c in range(8)])
        self.MSET(R('hst', S, S), 0.0)
        self.MSET(self.PDc('lnq', 0, 8), -2.772588722239781)
        self.MSET(R('hf', S, S, S, S), 0.0, writes=[('hf', l, j) for l in range(2) for j in range(24)])
        self.MSET(R('pu', S, S, S), 0.0, writes=[('pu', g) for g in range(4)])
        self.MSET(R('glu', S, S, S), 0.0, writes=[('glu', c) for c in range(8)])
        self.TS(self.PDc('ngxb', 0, 8), self.Pc('a_gate_x_b', 0, 8), -1.0, None, ALU.mult)
        self.TS(self.PDc('ngab', 0, 8), self.Pc('a_gate_a_b', 0, 8), -1.0, None, ALU.mult)
        self.TS(self.PDc('ncb1', 0, 8), self.Pc('c_b_pw1', 8, 8), -1.0, None, ALU.mult)
        s0, s1 = self.PDc('s0', 0, 8), self.PDc('s1', 0, 8)
        lam = self.Pc('a_lambda', 0, 8)
        self.A(AF.Abs, s0, lam)
        self.A(AF.Exp, s0, s0, scale=-1.0)
        self.A(AF.Ln, s0, s0, bias=1.0)
        self.A(AF.Relu, s1, lam, scale=-1.0)
        self.TT(s0, s0, s1, ALU.add)
        self.TS(self.PDc('cvec', 0, 8), s0, -8.0, None, ALU.mult)
        self.TS(self.PDc('c2vec', 0, 8), s0, -16.0, None, ALU.mult)
        for l in range(2):
            self.norm(lambda c: R('f32big', S, c, slice(0, 512)), 'xa_mem_norm%d' % l, n=NMEM, dst='xn')
            for n in range(8):
                b = self.balloc(NMEM)
                self.MM(b, [(self.W('mat', 'xa_wk', l, k * 128, n * 128), R('xn', S, k, slice(0, NMEM)))
                            for k in range(8)])
                self.A(AF.Copy, R('kT', S, l, n, S), b)
                self.bfree(b)
            for dh in range(2):
                b0, b1 = self.balloc(), self.balloc()
                for k in range(8):
                    w = self.W('mat', 'xa_wv', l, k * 128, dh * 512, nblk=4)
                    for bb_, lo in ((b0, 0), (b1, 128)):
                        def fn(e, bb_=bb_, lo=lo, k=k, w=w):
                            return e.matmul(bb_.ap(self.tm), lhsT=self.tm['xn'][:, k, lo:lo + 128], rhs=w.ap(self.tm),
                                            start=(k == 0), stop=(k == 7))
                        self.P.op('pe', fn, [('xn', k), w.key], [bb_.key])
                self.A(AF.Copy, R('v', S, l, 0, slice(dh * 512, dh * 512 + 512)), b0)
                self.A(AF.Copy, R('v', S, l, 1, slice(dh * 512, dh * 512 + 512)), b1)
                self.bfree(b0)
                self.bfree(b1)
        if self.known is None:
            self.ns_pad = -(-len(self.setup_blocks) // CH) * CH

    def load_x(self, t):
        par = t % 2
        self.DMA('pool', R('x', S, par, S, S), R('xT', S, S, slice(t * TT_, (t + 1) * TT_)), 'xl',
                 reads=[], writes=[('x', par, c) for c in range(8)])

    def mix_chunk(self, c):
        br = self.balloc()
        self.MM(br, [(self.W('mat', 'ab_w_in', 0, k * 128, 1024 + c * 128), self.xn(k)) for k in range(8)])
        yield
        bg = self.balloc()
        self.MM(bg, [(self.W('mat', 'ab_w_in', 0, k * 128, c * 128), self.xn(k)) for k in range(8)])
        yield
        acc = self.talloc()
        self.TS(acc, br, self.Pc('a_conv_w3', c), self.Pc('a_conv_b', c), ALU.mult, ALU.add)
        yield
        s_ = self.talloc()
        self.A(AF.Square, s_, bg, scale=0.044715 ** 0.5)
        for s in (1, 2, 3):
            wk = self.Pc('a_conv_w%d' % (3 - s), c)
            self.STT(acc.sub(s, 512), br.sub(0, 512 - s), wk, acc.sub(s, 512), ALU.mult, ALU.add)
            self.STT(acc.sub(0, s), R('hz', S, c, slice(3 - s, 3)), wk, acc.sub(0, s), ALU.mult, ALU.add)
            yield
        self.CP(R('hz', S, c, S), br.sub(509, 512))
        self.bfree(br)
        xrb = self.tballoc()
        self.CP(xrb, acc, eng='pool')
        yield
        self.STT(s_, s_, 1.0, bg, ALU.add, ALU.mult)
        yield
        self.A(AF.Exp, s_, s_, scale=-1.5957691216057308)
        yield
        bx = self.balloc()
        self.MM(bx, [(self.W('blk', 'a_gate_x_w', 0, c), xrb)])
        ba = self.balloc()
        self.MM(ba, [(self.W('blk', 'a_gate_a_w', 0, c), xrb)])
        self.tbfree(xrb)
        self.A(AF.Ln, s_, s_, bias=1.0)
        yield
        self.A(AF.Exp, s_, s_, scale=-1.0)
        yield
        sg = self.talloc()
        self.A(AF.Exp, sg, ba, scale=-1.0, bias=self.PDc('ngab', c))
        self.bfree(ba)
        self.TT(s_, s_, bg, ALU.mult)
        self.bfree(bg)
        yield
        gx = self.talloc()
        self.A(AF.Exp, gx, bx, scale=-1.0, bias=self.PDc('ngxb', c))
        self.bfree(bx)
        yield
        self.A(AF.Ln, sg, sg, bias=1.0)
        yield
        self.A(AF.Ln, gx, gx, bias=1.0)
        yield
        self.A(AF.Exp, sg, sg, scale=-1.0)
        yield
        self.A(AF.Exp, gx, gx, scale=-1.0)
        yield
        a = self.talloc()
        self.A(AF.Exp, a, sg, scale=self.PDc('cvec', c))
        self.tfree(sg)
        yield
        m = self.talloc()
        self.TT(m, a, a, ALU.mult, eng='pool')
        self.TT(gx, gx, acc, ALU.mult, eng='pool')
        self.tfree(acc)
        yield
        self.A(AF.Ln, m, m, scale=-(1.0 - 1e-6), bias=1.0)
        yield
        self.A(AF.Exp, m, m, scale=0.5)
        yield
        self.TT(gx, gx, m, ALU.mult, eng='pool')
        self.tfree(m)
        yield
        h = self.talloc()
        self.SCAN(h, a, gx, R('hst', S, slice(c, c + 1)))
        self.tfree(a)
        self.tfree(gx)
        yield
        self.CP(R('hst', S, slice(c, c + 1)), h.sub(511, 512))
        self.TT(R('big', S, c, S), s_, h, ALU.mult, eng='pool')
        self.tfree(h)
        self.tfree(s_)
        yield

    def pool_group(self, t, g):
        w = 2 ** (g + 1)
        bp = self.balloc()
        self.MM(bp, [(self.W('mat', 'ab_w_in', 0, k * 128, 2048 + g * 128), self.xn(k)) for k in range(8)])
        yield
        u = R('pu', S, g, slice(16, 528))
        self.A(AF.Copy, u, bp)
        self.bfree(bp)
        yield
        src = ('pu', g)
        m_ = 1
        for i in range(g + 1):
            dst = 'psA%d' % (g % 2) if i % 2 == 0 else 'psB%d' % (g % 2)
            lo = 2 * m_

            def mk(nm, a_, b_):
                return R('pu', S, g, slice(a_, b_)) if nm[0] == 'pu' else R(nm[0], S, slice(a_, b_))
            self.TT(R(dst, S, slice(lo, 528)), mk(src, lo, 528), mk(src, lo - m_, 528 - m_), ALU.add, eng='pool')
            src = (dst,)
            m_ *= 2
            yield
        for _ in range(8):
            yield
        sw = R(src[0], S, slice(16, 528))
        pb = self.tballoc()
        self.STT(pb, sw, 1.0 / w, u, ALU.mult, ALU.subtract)
        if t == 0 and w > 1:
            tc_ = self.talloc()
            self.TT(tc_.sub(0, w - 1), sw.sub(0, w - 1), self.Pc('pf', g * 16, w - 1), ALU.mult)
            self.TT(pb.sub(0, w - 1), tc_.sub(0, w - 1), u.sub(0, w - 1), ALU.subtract)
            self.tfree(tc_)
        self.CP(R('pu', S, g, slice(1, 16)), R('pu', S, g, slice(513, 528)))
        self.pb[g] = pb
        yield

    def pool_group_b(self, g):
        pb = self.pb.pop(g)
        bq = self.balloc()
        self.MM(bq, [(self.W('blk', 'b_group_w', 0, g), pb)])
        self.tbfree(pb)
        self.TS(R('big', S, 8 + g, S), bq, self.Pc('b_group_b', g), self.Pc('b_scale', g), ALU.add, ALU.mult)
        self.bfree(bq)

    def mixer(self, t, prenormed=False):
        if not prenormed:
            self.norm(self.X, 'ab_norm')
        extra = [R('f32big', S, c, slice(0, 512)) for c in range(8)]
        self.tfree_.extend(extra)
        ch = [self.mix_chunk(c) for c in range(8)]
        pg = [self.pool_group(t, g) for g in range(4)]
        gens = [ch[0], ch[1], pg[3], ch[2], pg[2], ch[3], pg[1], ch[4], pg[0], ch[5], ch[6], ch[7]]
        self.pb = {}
        chunks = [g for g in gens if g.gi_code.co_name == 'mix_chunk']
        ka = [0, 1, 2, 3, 4, 8, 9, 10, 11]
        kb = [5, 6, 7]

        def wout_pass(ks, first):
            if first:
                while len(self.pb) < 4:
                    yield
                for g in (3, 2, 1, 0):
                    self.pool_group_b(g)
                    yield
            for n in range(8):
                b = self.balloc()
                self.MM(b, [(self.W('mat', 'ab_w_out', 0, k * 128, n * 128), R('big', S, k, S)) for k in ks])
                self.resid_add(n, b)
                self.bfree(b)
                yield
        self.pipeline(gens, MIX_STAGGER, triggers={id(chunks[4]): [wout_pass(ka, True)]})
        self.tfree_ = [r for r in self.tfree_ if r.name != 'f32big']
        assert len(self.tfree_) == NTMP and len(self.bfree_) == 8 and len(self.tbfree_) == NTB
        for _ in wout_pass(kb, False):
            pass

    def xattn(self, l):
        for c in range(8):
            self.A(AF.Square, R('sq', S, c, S), self.X(c))
            self.A(AF.Identity, R('xn', S, c, S), self.X(c), scale=self.Pc('xa_norm%d' % l, c))
        bst = self.balloc()
        self.MM(bst, [(R('onesD', S, S), R('sq', S, c, S)) for c in range(8)])
        rs = self.talloc()
        self.A(AF.Ln, rs, bst, bias=EPS)
        self.bfree(bst)
        self.A(AF.Exp, rs, rs, scale=-0.5, bias=self.PDc('lnq', 0))
        for n in range(8):
            b = self.balloc()
            self.MM(b, [(self.W('mat', 'xa_wq', l, k * 128, n * 128), self.xn(k)) for k in range(8)])
            self.TT(R('big', S, n, S), b, rs, ALU.mult)
            self.bfree(b)
        self.tfree(rs)
        for hh in range(4):
            ex = []
            for mc in range(2):
                b = self.balloc()
                self.MM(b, [(R('kT', S, l, 2 * hh + dc, slice(mc * 128, (mc + 1) * 128)),
                             R('big', S, 2 * hh + dc, S)) for dc in range(2)])
                e = self.tballoc()
                self.A(AF.Exp, e, b)
                self.bfree(b)
                ex.append(e)
            bd = self.balloc()
            self.MM(bd, [(R('ones1', S, S), ex[mc]) for mc in range(2)])
            rd = self.talloc()
            self.A(AF.Ln, rd, bd)
            self.bfree(bd)
            self.A(AF.Exp, rd, rd, scale=-1.0)
            for dc in range(2):
                bo = self.balloc()
                d0 = (2 * hh + dc) * 128
                self.MM(bo, [(R('v', S, l, mc, slice(d0, d0 + 128)), ex[mc]) for mc in range(2)])
                self.TT(R('big', S, 8 + 2 * hh + dc, S), bo, rd, ALU.mult)
                self.bfree(bo)
            self.tfree(rd)
            for e in ex:
                self.tbfree(e)
        for n in range(8):
            b = self.balloc()
            self.MM(b, [(self.W('mat', 'xa_wo', l, k * 128, n * 128), R('big', S, 8 + k, S)) for k in range(8)])
            self.resid_add(n, b)
            self.bfree(b)

    def ffn_chunk(self, l, j):
        if j in self.ffn_pre:
            bg, bu = self.ffn_pre.pop(j)
        else:
            bg = self.balloc()
            self.MM(bg, [(self.W('mat', 'f_w_up', l, k * 128, j * 128), self.xn(k)) for k in range(8)])
            bu = self.balloc()
            self.MM(bu, [(self.W('mat', 'f_w_up', l, k * 128, 3072 + j * 128), self.xn(k)) for k in range(8)])
        yield
        acc = self.talloc()
        self.A(AF.Identity, acc, bg, scale=self.Pc('f_dw_w%d_2' % l, j), bias=self.Pc('f_dw_b%d' % l, j))
        yield
        for s in (1, 2):
            wk = self.Pc('f_dw_w%d_%d' % (l, 2 - s), j)
            self.STT(acc.sub(s, 512), bg.sub(0, 512 - s), wk, acc.sub(s, 512), ALU.mult, ALU.add)
            self.STT(acc.sub(0, s), R('hf', S, l, j, slice(2 - s, 2)), wk, acc.sub(0, s), ALU.mult, ALU.add)
        self.CP(R('hf', S, l, j, S), bg.sub(510, 512))
        self.bfree(bg)
        yield
        self.A(AF.Gelu_apprx_tanh, acc, acc)
        yield
        self.TT(R('big', S, j, S), acc, bu, ALU.mult)
        self.tfree(acc)
        self.bfree(bu)
        yield

    def ffn(self, l, hooks=None):
        self.norm(self.X, 'f_norm%d' % l)
        pre = [self.balloc() for _ in range(4)]
        self.MM_kouter(pre, [('f_w_up', l, 0), ('f_w_up', l, 3072), ('f_w_up', l, 128), ('f_w_up', l, 3072 + 128)])
        self.ffn_pre = {0: (pre[0], pre[1]), 1: (pre[2], pre[3])}
        self.pipeline([self.ffn_chunk(l, j) for j in range(24)], 1)
        for n in range(8):
            if hooks is not None and n in hooks:
                hooks[n]()
            b = self.balloc()
            self.MM(b, [(self.W('mat', 'f_w_down', l, k * 128, n * 128), R('big', S, k, S)) for k in range(24)])
            self.resid_add(n, b)
            self.bfree(b)

    def conformer(self):
        self.norm(self.X, 'c_norm')
        pre = [self.balloc() for _ in range(4)]
        self.MM_kouter(pre, [('c_w_pw1', 0, 0), ('c_w_pw1', 0, 1024), ('c_w_pw1', 0, 128), ('c_w_pw1', 0, 1024 + 128)])
        cpre = {0: (pre[0], pre[1]), 1: (pre[2], pre[3])}
        for c in range(8):
            if c in cpre:
                ba, bb = cpre.pop(c)
            else:
                ba = self.balloc()
                self.MM(ba, [(self.W('mat', 'c_w_pw1', 0, k * 128, c * 128), self.xn(k)) for k in range(8)])
                bb = self.balloc()
                self.MM(bb, [(self.W('mat', 'c_w_pw1', 0, k * 128, 1024 + c * 128), self.xn(k)) for k in range(8)])
            sg = self.talloc()
            self.sigmoid_from(sg, bb, self.PDc('ncb1', c))
            self.bfree(bb)
            self.STT(R('glu', S, c, slice(30, 542)), ba, self.Pc('c_b_pw1', c), sg, ALU.add, ALU.mult)
            self.tfree(sg)
            self.bfree(ba)
        for c in range(8):
            bc = self.balloc()
            self.MM(bc, [(self.W('diag', 'c_dw_w', 0, k, c), R('glu', S, c, slice(k, k + 512))) for k in range(31)])
            bia = self.Pc('c_dw_b', c)
            self.A(AF.Identity, R('f32big', S, c, S), bc, bias=bia)
            self.A(AF.Identity, R('xn', S, c, S), bc, bias=bia)
            self.A(AF.Square, R('sq', S, c, S), bc, bias=bia)
            self.bfree(bc)
            self.CP(R('glu', S, c, slice(0, 30)), R('glu', S, c, slice(512, 542)), eng='pool')
        bm = self.balloc()
        self.MM(bm, [(R('onesD', S, S), R('xn', S, c, S)) for c in range(8)])
        bq = self.balloc()
        self.MM(bq, [(R('onesD', S, S), R('sq', S, c, S)) for c in range(8)])
        msq = self.talloc()
        self.A(AF.Square, msq, bm)
        self.TT(bq, bq, msq, ALU.subtract)
        self.tfree(msq)
        self.A(AF.Ln, bq, bq, bias=EPS)
        self.A(AF.Exp, bq, bq, scale=-0.5)
        def ln_chunk(c):
            t2 = self.talloc()
            self.TT(t2, R('f32big', S, c, S), bm, ALU.subtract)
            yield
            self.TT(t2, t2, bq, ALU.mult)
            yield
            self.A(AF.Silu, R('big', S, c, S), t2, scale=self.Pc('c_ln_g', c), bias=self.Pc('c_ln_b', c))
            self.tfree(t2)
            yield
        self.pipeline([ln_chunk(c) for c in range(8)], 1)
        self.bfree(bm)
        self.bfree(bq)
        for n in range(8):
            b = self.balloc()
            self.MM(b, [(self.W('mat', 'c_w_pw2', 0, k * 128, n * 128), R('big', S, k, S)) for k in range(8)])
            x = self.X(n)
            self.STT(x, b, self.Pc('c_b_pw2', n), x, ALU.add, ALU.add)
            self.bfree(b)

    def trace(self):
        self.xp = 0
        self.setup()
        self.phase = 'tile'
        st = self.stages
        prenormed = False
        for t in range(self.nt):
            self.tile = t
            self.bpos = 0
            self.xp = t % 2
            self.pool_ok = t > 0
            more = t + 1 < self.nt
            if 'mix0' in st:
                self.mixer(t, prenormed)
            if more:
                self.load_x(t + 1)
            if 'xa0' in st:
                self.xattn(0)
            if 'ffn0' in st:
                self.ffn(0)
            if 'conf1' in st:
                self.conformer()
            if 'xa1' in st:
                self.xattn(1)
            prenormed = False
            if 'ffn1' in st:
                hooks = None
                if more and 'mix0' in st:
                    nx = (lambda c, par=(t + 1) % 2: self.X(c, par))
                    hold = {}
                    hooks = {1: (lambda: hold.__setitem__('b', self.norm_stats(nx))),
                             4: (lambda: self.norm_apply(hold['b'], nx, 'ab_norm'))}
                    prenormed = True
                self.ffn(1, hooks)
            if 'final' in st:
                self.norm(self.X, 'final_norm', dst='f32big')
                src = [R('f32big', S, c, S) for c in range(8)]
            else:
                src = [self.X(c) for c in range(8)]
            par = self.xp
            if more and self.known is not None:
                for q in range(NSLOT - 1):
                    self._ensure_chunk(self.ns_pad // CH + (t + 1) * (self.nb_pad // CH) + q, lookahead=0)

            def fn(e, t=t, par=par, final=('final' in st)):
                in_ = self.tm['f32big'][:, :, :] if final else self.tm['x'][:, par, :, :]
                return e.dma_start(out=self.tm['outT'][:, :, t * TT_:(t + 1) * TT_], in_=in_)
            self.P.op('sp', fn, [r.key for r in src], [('outT',)], dma='os')
            if t == 0 and self.known is None:
                self.nb_pad = -(-len(self.tile_blocks) // CH) * CH
        if self.known is None:
            self.ncols = (self.ns_pad + self.nb_pad) * 128

    def emit(self):
        nc = bass.Bass("TRN2", target_bir_lowering=False)
        tm = {}
        self.tm = tm
        seq = self.nt * TT_
        tm['xT'] = nc.dram_tensor("xT", [128, 8, seq], F32, kind="ExternalInput").ap()
        tm['memT'] = nc.dram_tensor("memT", [128, 8, NMEM], F32, kind="ExternalInput").ap()
        tm['wall'] = nc.dram_tensor("wall", [128, self.ncols], F32, kind="ExternalInput").ap()
        tm['params'] = nc.dram_tensor("params", [128, self.np_cols], F32, kind="ExternalInput").ap()
        tm['wbf'] = nc.dram_tensor("wbf", [128, self.ncols], BF16, kind="Internal").ap()
        tm['outT'] = nc.dram_tensor("outT", [128, 8, seq], F32, kind="ExternalOutput").ap()
        sb = [
            ('x', [128, 2, 8, 512], F32), ('f32big', [128, 8, 512], F32),
            ('sq', [128, 8, 512], BF16), ('xn', [128, 8, 512], BF16),
            ('big', [128, 24, 512], BF16), ('glu', [128, 8, 542], BF16),
            ('ring', [128, NSLOT * CH * 128], BF16),
            ('pu', [128, 4, 528], F32), ('psA0', [128, 528], F32), ('psB0', [128, 528], F32),
            ('psA1', [128, 528], F32), ('psB1', [128, 528], F32),
            ('kT', [128, 2, 8, NMEM], BF16), ('v', [128, 2, 2, 1024], BF16),
            ('P', [128, self.np_cols], F32), ('PD', [128, self.npd_cols], F32),
            ('hz', [128, 8, 3], F32), ('hst', [128, 8], F32), ('hf', [128, 2, 24, 2], F32),
            ('onesD', [128, 128], BF16), ('ones1', [128, 128], BF16),
        ]
        sb += [('tmp%d' % i, [128, 512], F32) for i in range(NTMP)]
        sb += [('tb%d' % i, [128, 512], BF16) for i in range(NTB)]
        with ExitStack() as es:
            for name, shape, dt in sb:
                tm[name] = es.enter_context(nc.sbuf_tensor(name, shape, dt))
            for i in range(8):
                tm['ps%d' % i] = es.enter_context(nc.psum_tensor('ps%d' % i, [128, 512], F32))
            sems = {n: es.enter_context(nc.semaphore(n)) for n in self.P.sem_names}
            block = es.enter_context(nc.Block())
            last_os = None
            for o in self.P.ops:
                if o.dma == 'os':
                    last_os = o.comp

            @block.tensor
            def _(e):
                self.P.emit_engine('pe', e, sems)

            @block.scalar
            def _(e):
                self.P.emit_engine('act', e, sems)

            @block.vector
            def _(e):
                self.P.emit_engine('dve', e, sems)

            @block.gpsimd
            def _(e):
                self.P.emit_engine('pool', e, sems)

            @block.sync
            def _(e):
                self.P.emit_engine('sp', e, sems)
                e.wait_ge(sems[last_os[0]], last_os[1])
        return nc

    def host_wall(self, inp):
        wall = np.zeros((128, self.ncols), np.float32)

        def fill(col, spec):
            kind = spec[0]
            if kind == 'mat':
                _, name, l, r0, c0 = spec[:5]
                j = spec[5] if len(spec) > 5 else 0
                blk = inp[name][l][r0:r0 + 128, c0 + j * 128:c0 + (j + 1) * 128]
            elif kind == 'blk':
                _, name, l, h = spec
                blk = inp[name][l][h]
            elif kind == 'diag':
                _, name, l, k, c = spec
                blk = np.diag(inp[name][l][k, c * 128:(c + 1) * 128])
            wall[:, col:col + 128] = blk
        for i, s in enumerate(self.setup_blocks):
            fill(i * 128, s)
        for i, s in enumerate(self.tile_blocks):
            fill((self.ns_pad + i) * 128, s)
        return wall

    def host_params(self, inp):
        Pm = np.zeros((128, self.np_cols), np.float32)

        def put(name, vec):
            vec = np.asarray(vec, np.float32).reshape(-1, 128)
            c = self.pcols[name]
            Pm[:, c:c + vec.shape[0]] = vec.T
        put('ab_norm', inp['ab_norm'][0])
        for k in range(4):
            put('a_conv_w%d' % k, inp['a_conv_w'][0][k])
        for n in ('a_conv_b', 'a_gate_x_b', 'a_gate_a_b', 'a_lambda', 'b_group_b', 'b_scale',
                  'c_norm', 'c_b_pw1', 'c_dw_b', 'c_ln_g', 'c_ln_b', 'c_b_pw2'):
            put(n, inp[n][0])
        for l in range(2):
            put('xa_norm%d' % l, inp['xa_norm'][l])
            put('xa_mem_norm%d' % l, inp['xa_mem_norm'][l])
            put('f_norm%d' % l, inp['f_norm'][l])
            for k in range(3):
                put('f_dw_w%d_%d' % (l, k), inp['f_dw_w'][l][k])
            put('f_dw_b%d' % l, inp['f_dw_b'][l])
        put('final_norm', inp['final_norm'])
        c = self.pcols['pf']
        for g in range(4):
            w = 2 ** (g + 1)
            for t in range(16):
                Pm[:, c + g * 16 + t] = 1.0 / min(t + 1, w)
        return Pm


_CACHE = {}


def get_builder(nt=SEQ // TT_, stages=ALL_STAGES):
    key = (nt, tuple(stages))
    if key not in _CACHE:
        dry = Builder(1, stages)
        _CACHE[key] = Builder(nt, stages, known=(dry.setup_blocks, dry.tile_blocks))
    return _CACHE[key]


def run(inp, nt=SEQ // TT_, stages=ALL_STAGES, ncores=8, trace=False):
    B = get_builder(nt, stages)
    inp = {k: np.asarray(v) for k, v in inp.items()}
    nc = B.emit()
    wall = B.host_wall(inp)
    params = B.host_params(inp)
    seq = nt * TT_
    in_maps = []
    for i in range(ncores):
        xT = np.ascontiguousarray(inp['x'][i, :seq].T.reshape(8, 128, seq).transpose(1, 0, 2))
        memT = np.ascontiguousarray(inp['mem'][i].T.reshape(8, 128, NMEM).transpose(1, 0, 2))
        in_maps.append({'xT': xT, 'memT': memT, 'wall': wall, 'params': params})
    res = run_bass_kernel_spmd(nc, in_maps, core_ids=list(range(ncores)), trace=trace)
    outs = []
    for i in range(ncores):
        o = np.asarray(res.results[i]['outT'])
        outs.append(o.transpose(1, 0, 2).reshape(1024, seq).T)
    return np.stack(outs).astype(np.float32), res


def kernel(**inputs):
    out, _ = run(inputs)
    return out
```

```python
import numpy as np
from contextlib import ExitStack
import concourse.bass as bass
import concourse.mybir as mybir
from concourse.bass_utils import run_bass_kernel_spmd

F32, BF16 = mybir.dt.float32, mybir.dt.bfloat16
AF = mybir.ActivationFunctionType
ALU = mybir.AluOpType
S = slice(None)
TT_ = 512
SEQ = 4096
NMEM = 256
CH = 32
NSLOT = 4
PIECE = 4
EPS = 1e-6
NTMP = 12
NTB = 8
MIX_STAGGER = 5
ALL_STAGES = ('mix0', 'xa0', 'ffn0', 'conf1', 'xa1', 'ffn1', 'final')


class R:
    __slots__ = ('name', 'idx', 'key')

    def __init__(self, name, *idx, key=None):
        self.name = name
        self.idx = idx
        self.key = key if key is not None else (name,) + tuple(i for i in idx if isinstance(i, int))

    def ap(self, tm):
        t = tm[self.name]
        return t[self.idx] if self.idx else t[:]

    def sub(self, a, b):
        idx = list(self.idx)
        last = idx[-1]
        start = 0 if last.start is None else last.start
        idx[-1] = slice(start + a, start + b)
        return R(self.name, *idx, key=self.key)


class Op:
    __slots__ = ('eng', 'fn', 'deps', 'dma', 'sig', 'comp')

    def __init__(self, eng, fn, deps, dma):
        self.eng, self.fn, self.deps, self.dma = eng, fn, deps, dma
        self.sig = False
        self.comp = None


class Prog:
    def __init__(self):
        self.ops = []
        self.last_w = {}
        self.readers = {}

    def op(self, eng, fn, reads=(), writes=(), dma=None):
        idx = len(self.ops)
        deps = {}
        for k in reads:
            w = self.last_w.get(k)
            if w is not None:
                deps[w] = 'RAW'
        for k in writes:
            w = self.last_w.get(k)
            if w is not None:
                deps.setdefault(w, 'WAW')
            for r in self.readers.get(k, {}).values():
                deps.setdefault(r, 'WAR')
        chan = dma if dma is not None else eng
        for k in reads:
            self.readers.setdefault(k, {})[chan] = idx
        for k in writes:
            self.last_w[k] = idx
            self.readers[k] = {}
        self.ops.append(Op(eng, fn, deps, dma))
        return idx

    def finalize(self):
        ops = self.ops
        for o in ops:
            for d, typ in o.deps.items():
                p = ops[d]
                if p.dma is not None:
                    continue
                if o.dma is not None or p.eng != o.eng or typ == 'RAW':
                    p.sig = True
        cnt = {}
        for o in ops:
            if o.dma is not None:
                cnt[o.dma] = cnt.get(o.dma, 0) + 16
                o.comp = ('dma_' + o.dma, cnt[o.dma])
            elif o.sig:
                cnt[o.eng] = cnt.get(o.eng, 0) + 1
                o.comp = ('eng_' + o.eng, cnt[o.eng])
        self.sem_names = sorted(set(o.comp[0] for o in ops if o.comp is not None))

    def emit_engine(self, engname, eng, sems):
        ops = self.ops
        waited = {}
        for o in ops:
            if o.eng != engname:
                continue
            for d, typ in o.deps.items():
                p = ops[d]
                if p.dma is None and not (o.dma is not None or p.eng != o.eng or typ == 'RAW'):
                    continue
                sname, val = p.comp
                if waited.get(sname, 0) >= val:
                    continue
                waited[sname] = val
                eng.wait_ge(sems[sname], val)
            ins = o.fn(eng)
            if o.comp is not None:
                ins.then_inc(sems[o.comp[0]], 16 if o.dma is not None else 1)
        return waited


class Builder:
    def __init__(self, nt=SEQ // TT_, stages=ALL_STAGES, known=None):
        self.nt = nt
        self.stages = tuple(stages)
        self.P = Prog()
        self.setup_blocks = []
        self.tile_blocks = []
        self.known = known
        self.pool_ok = False
        if known is not None:
            self.ns_pad = -(-len(known[0]) // CH) * CH
            self.nb_pad = -(-len(known[1]) // CH) * CH
            self.nb_real_ch = -(-len(known[1]) // CH)
            self.ncols = (self.ns_pad + self.nb_pad) * 128
        else:
            self.nb_pad = None
            self.ns_pad = None
        self.chunks_issued = set()
        self.pieces_cast = set()
        self.phase = 'setup'
        self.tile = -1
        self.bpos = 0
        self.bfree_ = [R('ps%d' % i, S, slice(0, 512)) for i in range(8)]
        self.tfree_ = [R('tmp%d' % i, S, slice(0, 512)) for i in range(NTMP)]
        self.tbfree_ = [R('tb%d' % i, S, slice(0, 512)) for i in range(NTB)]
        self.pcols = {}
        self.np_cols = 0
        self.pdcols = {}
        self.npd_cols = 0
        self._param_layout()
        self.trace()
        self.P.finalize()

    def _padd(self, name, n):
        self.pcols[name] = self.np_cols
        self.np_cols += n

    def _pdadd(self, name, n):
        self.pdcols[name] = self.npd_cols
        self.npd_cols += n

    def _param_layout(self):
        a = self._padd
        a('ab_norm', 8)
        for k in range(4):
            a('a_conv_w%d' % k, 8)
        for n in ('a_conv_b', 'a_gate_x_b', 'a_gate_a_b', 'a_lambda'):
            a(n, 8)
        a('b_group_b', 4)
        a('b_scale', 4)
        a('c_norm', 8)
        a('c_b_pw1', 16)
        for n in ('c_dw_b', 'c_ln_g', 'c_ln_b', 'c_b_pw2'):
            a(n, 8)
        for l in range(2):
            a('xa_norm%d' % l, 8)
            a('xa_mem_norm%d' % l, 8)
            a('f_norm%d' % l, 8)
            for k in range(3):
                a('f_dw_w%d_%d' % (l, k), 24)
            a('f_dw_b%d' % l, 24)
        a('final_norm', 8)
        a('pf', 64)
        for n in ('ngxb', 'ngab', 'ncb1', 'cvec', 'c2vec', 's0', 's1', 'lnq'):
            self._pdadd(n, 8)

    def Pc(self, name, i=0, n=1):
        c = self.pcols[name] + i
        return R('P', S, slice(c, c + n), key=('P',))

    def PDc(self, name, i=0, n=1):
        c = self.pdcols[name] + i
        return R('PD', S, slice(c, c + n), key=('PD', name))

    def balloc(self, n=512):
        assert self.bfree_, 'PSUM banks exhausted'
        b = self.bfree_.pop(0)
        return b if n == 512 else b.sub(0, n)

    def bfree(self, b):
        self.bfree_.append(R(b.name, S, slice(0, 512)))

    def talloc(self, n=512):
        assert self.tfree_, 'tmp pool exhausted'
        t = self.tfree_.pop(0)
        return t if n == 512 else t.sub(0, n)

    def tfree(self, t):
        idx = list(t.idx)
        idx[-1] = slice(0, 512)
        self.tfree_.append(R(t.name, *idx))

    def tballoc(self):
        assert self.tbfree_, 'bf16 tmp pool exhausted'
        return self.tbfree_.pop(0)

    def tbfree(self, t):
        self.tbfree_.append(R(t.name, S, slice(0, 512)))

    def pipeline(self, gens, stagger, triggers=None):
        active, nxt, rnd = [], 0, 0
        triggers = triggers or {}
        while nxt < len(gens) or active:
            if nxt < len(gens) and rnd >= nxt * stagger:
                active.append(gens[nxt])
                nxt += 1
            for g in list(active):
                try:
                    next(g)
                except StopIteration:
                    active.remove(g)
                    active.extend(triggers.get(id(g), []))
            rnd += 1

    def W(self, *spec, nblk=1):
        if self.phase == 'setup':
            pos = len(self.setup_blocks)
            for j in range(nblk):
                self.setup_blocks.append(spec + ((j,) if nblk > 1 else ()))
            gb = pos
        else:
            pos = self.bpos
            for j in range(nblk):
                s = spec + ((j,) if nblk > 1 else ())
                if self.tile == 0:
                    self.tile_blocks.append(s)
                else:
                    assert self.tile_blocks[pos + j] == s
            self.bpos += nblk
            nbp = self.nb_pad if (self.tile > 0) else 0
            gb = self.ns_pad + self.tile * nbp + pos
        g0, g1 = gb // CH, (gb + nblk - 1) // CH
        assert g0 == g1
        self._ensure_chunk(g0)
        slot = g0 % NSLOT
        off = (slot * CH + gb % CH) * 128
        return R('ring', S, slice(off, off + 128 * nblk), key=('ring', slot))

    def _ensure_chunk(self, g, lookahead=2):
        if g in self.chunks_issued:
            return
        self.chunks_issued.add(g)
        slot = g % NSLOT
        if self.known is None:
            self.P.op('sp', lambda e: None, reads=[], writes=[('ring', slot)], dma='ring%d' % slot)
            return
        nsc = self.ns_pad // CH
        nch = self.nb_pad // CH
        first_pass = g < nsc + nch
        if first_pass:
            dcol = g * CH * 128

            def ldfn(e, dcol=dcol, slot=slot):
                return e.dma_start(
                    out=self.tm['ring'][:, slot * CH * 128:(slot + 1) * CH * 128].rearrange("p (n e) -> p n e", e=1024),
                    in_=self.tm['wall'][:, dcol:dcol + CH * 128].rearrange("p (n e) -> p n e", e=1024))
            self.P.op('pool', ldfn, reads=[], writes=[('ring', slot)], dma='ring%d' % slot)

            def wbfn(e, dcol=dcol, slot=slot):
                return e.dma_start(out=self.tm['wbf'][:, dcol:dcol + CH * 128],
                                   in_=self.tm['ring'][:, slot * CH * 128:(slot + 1) * CH * 128])
            if g >= nsc:
                self.P.op('sp', wbfn, reads=[('ring', slot)], writes=[('wbf', g)], dma='wb%d' % slot)
            for k in range(1, lookahead + 1):
                if g + k < nsc + self.nb_real_ch:
                    self._ensure_chunk(g + k, lookahead=0)
        else:
            q = (g - nsc) % nch
            dcol = (self.ns_pad + q * CH) * 128

            def ldfn(e, dcol=dcol, slot=slot):
                return e.dma_start(out=self.tm['ring'][:, slot * CH * 128:(slot + 1) * CH * 128],
                                   in_=self.tm['wbf'][:, dcol:dcol + CH * 128])
            self.P.op('sp', ldfn, reads=[('wbf', nsc + q)], writes=[('ring', slot)], dma='ring%d' % slot)

    def _sc(self, v):
        return v.ap(self.tm) if isinstance(v, R) else v

    def A(self, func, out, in_, scale=None, bias=None):
        reads = [in_.key] + [v.key for v in (scale, bias) if isinstance(v, R)]

        def fn(e):
            kw = {}
            if scale is not None:
                kw['scale'] = self._sc(scale)
            if bias is not None:
                kw['bias'] = self._sc(bias)
            return e.activation(out=out.ap(self.tm), in_=in_.ap(self.tm), func=func, **kw)
        self.P.op('act', fn, reads, [out.key])

    def TT(self, out, in0, in1, op, eng='dve'):
        if eng == 'pool' and not self.pool_ok:
            eng = 'dve'

        def fn(e):
            return e.tensor_tensor(out=out.ap(self.tm), in0=in0.ap(self.tm), in1=in1.ap(self.tm), op=op)
        self.P.op(eng, fn, [in0.key, in1.key], [out.key])

    def TS(self, out, in0, s1, s2, op0, op1=None, eng='dve'):
        reads = [in0.key] + [v.key for v in (s1, s2) if isinstance(v, R)]

        def fn(e):
            if op1 is None:
                return e.tensor_scalar(out=out.ap(self.tm), in0=in0.ap(self.tm), scalar1=self._sc(s1),
                                       scalar2=None, op0=op0)
            return e.tensor_scalar(out=out.ap(self.tm), in0=in0.ap(self.tm), scalar1=self._sc(s1),
                                   scalar2=self._sc(s2), op0=op0, op1=op1)
        self.P.op(eng, fn, reads, [out.key])

    def STT(self, out, in0, scalar, in1, op0, op1):
        reads = [in0.key, in1.key] + ([scalar.key] if isinstance(scalar, R) else [])

        def fn(e):
            return e.scalar_tensor_tensor(out=out.ap(self.tm), in0=in0.ap(self.tm), scalar=self._sc(scalar),
                                          in1=in1.ap(self.tm), op0=op0, op1=op1)
        self.P.op('dve', fn, reads, [out.key])

    def SCAN(self, out, d0, d1, initial):
        def fn(e):
            return e.tensor_tensor_scan(out=out.ap(self.tm), data0=d0.ap(self.tm), data1=d1.ap(self.tm),
                                        initial=initial.ap(self.tm), op0=ALU.mult, op1=ALU.add)
        self.P.op('dve', fn, [d0.key, d1.key, initial.key], [out.key])

    def CP(self, out, in_, eng='dve'):
        if eng == 'pool' and not self.pool_ok:
            return self.A(AF.Copy, out, in_)

        def fn(e):
            return e.tensor_copy(out=out.ap(self.tm), in_=in_.ap(self.tm))
        self.P.op(eng, fn, [in_.key], [out.key])

    def MSET(self, out, val, eng='dve', writes=None):
        def fn(e):
            return e.memset(out.ap(self.tm), val)
        self.P.op(eng, fn, [], writes if writes is not None else [out.key])

    def MM(self, bank, pairs):
        n = len(pairs)
        for i, (l, r) in enumerate(pairs):
            def fn(e, l=l, r=r, i=i):
                return e.matmul(bank.ap(self.tm), lhsT=l.ap(self.tm), rhs=r.ap(self.tm),
                                start=(i == 0), stop=(i == n - 1))
            self.P.op('pe', fn, [l.key, r.key], [bank.key])

    def MM_kouter(self, banks, wspecs):
        for k in range(8):
            for b, (nm, l, c0) in zip(banks, wspecs):
                w = self.W('mat', nm, l, k * 128, c0)
                r = self.xn(k)

                def fn(e, b=b, w=w, r=r, k=k):
                    return e.matmul(b.ap(self.tm), lhsT=w.ap(self.tm), rhs=r.ap(self.tm),
                                    start=(k == 0), stop=(k == 7))
                self.P.op('pe', fn, [w.key, r.key], [b.key])

    def DMA(self, eng, out, in_, chan, reads=None, writes=None):
        def fn(e):
            return e.dma_start(out=out.ap(self.tm), in_=in_.ap(self.tm))
        self.P.op(eng, fn, reads if reads is not None else [in_.key],
                  writes if writes is not None else [out.key], dma=chan)

    def norm_stats(self, srcf, n=512):
        for c in range(8):
            self.A(AF.Square, R('sq', S, c, slice(0, n)), srcf(c).sub(0, n))
        b = self.balloc(n)
        self.MM(b, [(R('onesD', S, S), R('sq', S, c, slice(0, n))) for c in range(8)])
        self.A(AF.Ln, b, b, bias=EPS)
        self.A(AF.Exp, b, b, scale=-0.5)
        return b

    def norm_apply(self, b, srcf, gname, n=512, dst='xn'):
        for c in range(8):
            self.STT(R(dst, S, c, slice(0, n)), srcf(c).sub(0, n), self.Pc(gname, c), b, ALU.mult, ALU.mult)
        self.bfree(b)

    def norm(self, srcf, gname, n=512, dst='xn'):
        b = self.norm_stats(srcf, n)
        self.norm_apply(b, srcf, gname, n, dst)

    def sigmoid_from(self, out, in_, nbias=None):
        self.A(AF.Exp, out, in_, scale=-1.0, bias=nbias)
        self.A(AF.Ln, out, out, bias=1.0)
        self.A(AF.Exp, out, out, scale=-1.0)

    def X(self, c, par=None):
        return R('x', S, self.xp if par is None else par, c, S)

    def resid_add(self, n, bank):
        x = self.X(n)
        self.TT(x, bank, x, ALU.add)

    def xn(self, k):
        return R('xn', S, k, S)

    def setup(self):
        self.phase = 'setup'
        self.DMA('pool', R('P', S, S, key=('P',)), R('params', S, S), 'pl')
        self.DMA('pool', R('f32big', S, S, slice(0, NMEM)), R('memT', S, S, S), 'ml',
                 writes=[('f32big', c) for c in range(8)])
        self.load_x(0)
        self.MSET(R('onesD', S, S), 1.0 / 1024.0)
        self.MSET(R('ones1', S, S), 1.0)
        self.MSET(R('hz', S, S, S), 0.0, writes=[('hz', c) for c in range(8)])
        self.MSET(R('hst', S, S), 0.0)
        self.MSET(self.PDc('lnq', 0, 8), -2.772588722239781)
        self.MSET(R('hf', S, S, S, S), 0.0, writes=[('hf', l, j) for l in range(2) for j in range(24)])
        self.MSET(R('pu', S, S, S), 0.0, writes=[('pu', g) for g in range(4)])
        self.MSET(R('glu', S, S, S), 0.0, writes=[('glu', c) for c in range(8)])
        self.TS(self.PDc('ngxb', 0, 8), self.Pc('a_gate_x_b', 0, 8), -1.0, None, ALU.mult)
        self.TS(self.PDc('ngab', 0, 8), self.Pc('a_gate_a_b', 0, 8), -1.0, None, ALU.mult)
        self.TS(self.PDc('ncb1', 0, 8), self.Pc('c_b_pw1', 8, 8), -1.0, None, ALU.mult)
        s0, s1 = self.PDc('s0', 0, 8), self.PDc('s1', 0, 8)
        lam = self.Pc('a_lambda', 0, 8)
        self.A(AF.Abs, s0, lam)
        self.A(AF.Exp, s0, s0, scale=-1.0)
        self.A(AF.Ln, s0, s0, bias=1.0)
        self.A(AF.Relu, s1, lam, scale=-1.0)
        self.TT(s0, s0, s1, ALU.add)
        self.TS(self.PDc('cvec', 0, 8), s0, -8.0, None, ALU.mult)
        self.TS(self.PDc('c2vec', 0, 8), s0, -16.0, None, ALU.mult)
        for l in range(2):
            self.norm(lambda c: R('f32big', S, c, slice(0, 512)), 'xa_mem_norm%d' % l, n=NMEM, dst='xn')
            for n in range(8):
                b = self.balloc(NMEM)
                self.MM(b, [(self.W('mat', 'xa_wk', l, k * 128, n * 128), R('xn', S, k, slice(0, NMEM)))
                            for k in range(8)])
                self.A(AF.Copy, R('kT', S, l, n, S), b)
                self.bfree(b)
            for dh in range(2):
                b0, b1 = self.balloc(), self.balloc()
                for k in range(8):
                    w = self.W('mat', 'xa_wv', l, k * 128, dh * 512, nblk=4)
                    for bb_, lo in ((b0, 0), (b1, 128)):
                        def fn(e, bb_=bb_, lo=lo, k=k, w=w):
                            return e.matmul(bb_.ap(self.tm), lhsT=self.tm['xn'][:, k, lo:lo + 128], rhs=w.ap(self.tm),
                                            start=(k == 0), stop=(k == 7))
                        self.P.op('pe', fn, [('xn', k), w.key], [bb_.key])
                self.A(AF.Copy, R('v', S, l, 0, slice(dh * 512, dh * 512 + 512)), b0)
                self.A(AF.Copy, R('v', S, l, 1, slice(dh * 512, dh * 512 + 512)), b1)
                self.bfree(b0)
                self.bfree(b1)
        if self.known is None:
            self.ns_pad = -(-len(self.setup_blocks) // CH) * CH

    def load_x(self, t):
        par = t % 2
        self.DMA('pool', R('x', S, par, S, S), R('xT', S, S, slice(t * TT_, (t + 1) * TT_)), 'xl',
                 reads=[], writes=[('x', par, c) for c in range(8)])

    def mix_chunk(self, c):
        br = self.balloc()
        self.MM(br, [(self.W('mat', 'ab_w_in', 0, k * 128, 1024 + c * 128), self.xn(k)) for k in range(8)])
        yield
        bg = self.balloc()
        self.MM(bg, [(self.W('mat', 'ab_w_in', 0, k * 128, c * 128), self.xn(k)) for k in range(8)])
        yield
        acc = self.talloc()
        self.TS(acc, br, self.Pc('a_conv_w3', c), self.Pc('a_conv_b', c), ALU.mult, ALU.add)
        yield
        s_ = self.talloc()
        self.A(AF.Square, s_, bg, scale=0.044715 ** 0.5)
        for s in (1, 2, 3):
            wk = self.Pc('a_conv_w%d' % (3 - s), c)
            self.STT(acc.sub(s, 512), br.sub(0, 512 - s), wk, acc.sub(s, 512), ALU.mult, ALU.add)
            self.STT(acc.sub(0, s), R('hz', S, c, slice(3 - s, 3)), wk, acc.sub(0, s), ALU.mult, ALU.add)
            yield
        self.CP(R('hz', S, c, S), br.sub(509, 512))
        self.bfree(br)
        xrb = self.tballoc()
        self.CP(xrb, acc, eng='pool')
        yield
        self.STT(s_, s_, 1.0, bg, ALU.add, ALU.mult)
        yield
        self.A(AF.Exp, s_, s_, scale=-1.5957691216057308)
        yield
        bx = self.balloc()
        self.MM(bx, [(self.W('blk', 'a_gate_x_w', 0, c), xrb)])
        ba = self.balloc()
        self.MM(ba, [(self.W('blk', 'a_gate_a_w', 0, c), xrb)])
        self.tbfree(xrb)
        self.A(AF.Ln, s_, s_, bias=1.0)
        yield
        self.A(AF.Exp, s_, s_, scale=-1.0)
        yield
        sg = self.talloc()
        self.A(AF.Exp, sg, ba, scale=-1.0, bias=self.PDc('ngab', c))
        self.bfree(ba)
        self.TT(s_, s_, bg, ALU.mult)
        self.bfree(bg)
        yield
        gx = self.talloc()
        self.A(AF.Exp, gx, bx, scale=-1.0, bias=self.PDc('ngxb', c))
        self.bfree(bx)
        yield
        self.A(AF.Ln, sg, sg, bias=1.0)
        yield
        self.A(AF.Ln, gx, gx, bias=1.0)
        yield
        self.A(AF.Exp, sg, sg, scale=-1.0)
        yield
        self.A(AF.Exp, gx, gx, scale=-1.0)
        yield
        a = self.talloc()
        self.A(AF.Exp, a, sg, scale=self.PDc('cvec', c))
        self.tfree(sg)
        yield
        m = self.talloc()
        self.TT(m, a, a, ALU.mult, eng='pool')
        self.TT(gx, gx, acc, ALU.mult, eng='pool')
        self.tfree(acc)
        yield
        self.A(AF.Ln, m, m, scale=-(1.0 - 1e-6), bias=1.0)
        yield
        self.A(AF.Exp, m, m, scale=0.5)
        yield
        self.TT(gx, gx, m, ALU.mult)
        self.tfree(m)
        yield
        h = self.talloc()
        self.SCAN(h, a, gx, R('hst', S, slice(c, c + 1)))
        self.tfree(a)
        self.tfree(gx)
        yield
        self.CP(R('hst', S, slice(c, c + 1)), h.sub(511, 512))
        self.TT(R('big', S, c, S), s_, h, ALU.mult)
        self.tfree(h)
        self.tfree(s_)
        yield

    def pool_group(self, t, g):
        w = 2 ** (g + 1)
        bp = self.balloc()
        self.MM(bp, [(self.W('mat', 'ab_w_in', 0, k * 128, 2048 + g * 128), self.xn(k)) for k in range(8)])
        yield
        u = R('pu', S, g, slice(16, 528))
        self.A(AF.Copy, u, bp)
        self.bfree(bp)
        yield
        src = ('pu', g)
        m_ = 1
        for i in range(g + 1):
            dst = 'psA%d' % (g % 2) if i % 2 == 0 else 'psB%d' % (g % 2)
            lo = 2 * m_

            def mk(nm, a_, b_):
                return R('pu', S, g, slice(a_, b_)) if nm[0] == 'pu' else R(nm[0], S, slice(a_, b_))
            self.TT(R(dst, S, slice(lo, 528)), mk(src, lo, 528), mk(src, lo - m_, 528 - m_), ALU.add, eng='pool')
            src = (dst,)
            m_ *= 2
            yield
        for _ in range(8):
            yield
        sw = R(src[0], S, slice(16, 528))
        pb = self.tballoc()
        self.STT(pb, sw, 1.0 / w, u, ALU.mult, ALU.subtract)
        if t == 0 and w > 1:
            tc_ = self.talloc()
            self.TT(tc_.sub(0, w - 1), sw.sub(0, w - 1), self.Pc('pf', g * 16, w - 1), ALU.mult)
            self.TT(pb.sub(0, w - 1), tc_.sub(0, w - 1), u.sub(0, w - 1), ALU.subtract)
            self.tfree(tc_)
        self.CP(R('pu', S, g, slice(1, 16)), R('pu', S, g, slice(513, 528)))
        self.pb[g] = pb
        yield

    def pool_group_b(self, g):
        pb = self.pb.pop(g)
        bq = self.balloc()
        self.MM(bq, [(self.W('blk', 'b_group_w', 0, g), pb)])
        self.tbfree(pb)
        self.TS(R('big', S, 8 + g, S), bq, self.Pc('b_group_b', g), self.Pc('b_scale', g), ALU.add, ALU.mult)
        self.bfree(bq)

    def mixer(self, t, prenormed=False):
        if not prenormed:
            self.norm(self.X, 'ab_norm')
        extra = [R('f32big', S, c, slice(0, 512)) for c in range(8)]
        self.tfree_.extend(extra)
        ch = [self.mix_chunk(c) for c in range(8)]
        pg = [self.pool_group(t, g) for g in range(4)]
        gens = [ch[0], ch[1], pg[3], ch[2], pg[2], ch[3], pg[1], ch[4], pg[0], ch[5], ch[6], ch[7]]
        self.pb = {}
        chunks = [g for g in gens if g.gi_code.co_name == 'mix_chunk']
        ka = [0, 1, 2, 3, 4, 8, 9, 10, 11]
        kb = [5, 6, 7]

        def wout_pass(ks, first):
            if first:
                while len(self.pb) < 4:
                    yield
                for g in (3, 2, 1, 0):
                    self.pool_group_b(g)
                    yield
            for n in range(8):
                b = self.balloc()
                self.MM(b, [(self.W('mat', 'ab_w_out', 0, k * 128, n * 128), R('big', S, k, S)) for k in ks])
                self.resid_add(n, b)
                self.bfree(b)
                yield
        self.pipeline(gens, MIX_STAGGER, triggers={id(chunks[4]): [wout_pass(ka, True)]})
        self.tfree_ = [r for r in self.tfree_ if r.name != 'f32big']
        assert len(self.tfree_) == NTMP and len(self.bfree_) == 8 and len(self.tbfree_) == NTB
        for _ in wout_pass(kb, False):
            pass

    def xattn(self, l):
        for c in range(8):
            self.A(AF.Square, R('sq', S, c, S), self.X(c))
            self.A(AF.Identity, R('xn', S, c, S), self.X(c), scale=self.Pc('xa_norm%d' % l, c))
        bst = self.balloc()
        self.MM(bst, [(R('onesD', S, S), R('sq', S, c, S)) for c in range(8)])
        rs = self.talloc()
        self.A(AF.Ln, rs, bst, bias=EPS)
        self.bfree(bst)
        self.A(AF.Exp, rs, rs, scale=-0.5, bias=self.PDc('lnq', 0))
        for n in range(8):
            b = self.balloc()
            self.MM(b, [(self.W('mat', 'xa_wq', l, k * 128, n * 128), self.xn(k)) for k in range(8)])
            self.TT(R('big', S, n, S), b, rs, ALU.mult)
            self.bfree(b)
        self.tfree(rs)
        for hh in range(4):
            ex = []
            for mc in range(2):
                b = self.balloc()
                self.MM(b, [(R('kT', S, l, 2 * hh + dc, slice(mc * 128, (mc + 1) * 128)),
                             R('big', S, 2 * hh + dc, S)) for dc in range(2)])
                e = self.tballoc()
                self.A(AF.Exp, e, b)
                self.bfree(b)
                ex.append(e)
            bd = self.balloc()
            self.MM(bd, [(R('ones1', S, S), ex[mc]) for mc in range(2)])
            rd = self.talloc()
            self.A(AF.Ln, rd, bd)
            self.bfree(bd)
            self.A(AF.Exp, rd, rd, scale=-1.0)
            for dc in range(2):
                bo = self.balloc()
                d0 = (2 * hh + dc) * 128
                self.MM(bo, [(R('v', S, l, mc, slice(d0, d0 + 128)), ex[mc]) for mc in range(2)])
                self.TT(R('big', S, 8 + 2 * hh + dc, S), bo, rd, ALU.mult)
                self.bfree(bo)
            self.tfree(rd)
            for e in ex:
                self.tbfree(e)
        for n in range(8):
            b = self.balloc()
            self.MM(b, [(self.W('mat', 'xa_wo', l, k * 128, n * 128), R('big', S, 8 + k, S)) for k in range(8)])
            self.resid_add(n, b)
            self.bfree(b)

    def ffn_chunk(self, l, j):
        if j in self.ffn_pre:
            bg, bu = self.ffn_pre.pop(j)
        else:
            bg = self.balloc()
            self.MM(bg, [(self.W('mat', 'f_w_up', l, k * 128, j * 128), self.xn(k)) for k in range(8)])
            bu = self.balloc()
            self.MM(bu, [(self.W('mat', 'f_w_up', l, k * 128, 3072 + j * 128), self.xn(k)) for k in range(8)])
        yield
        acc = self.talloc()
        self.A(AF.Identity, acc, bg, scale=self.Pc('f_dw_w%d_2' % l, j), bias=self.Pc('f_dw_b%d' % l, j))
        yield
        for s in (1, 2):
            wk = self.Pc('f_dw_w%d_%d' % (l, 2 - s), j)
            self.STT(acc.sub(s, 512), bg.sub(0, 512 - s), wk, acc.sub(s, 512), ALU.mult, ALU.add)
            self.STT(acc.sub(0, s), R('hf', S, l, j, slice(2 - s, 2)), wk, acc.sub(0, s), ALU.mult, ALU.add)
        self.CP(R('hf', S, l, j, S), bg.sub(510, 512))
        self.bfree(bg)
        yield
        self.A(AF.Gelu_apprx_tanh, acc, acc)
        yield
        self.TT(R('big', S, j, S), acc, bu, ALU.mult)
        self.tfree(acc)
        self.bfree(bu)
        yield

    def ffn(self, l, hooks=None):
        self.norm(self.X, 'f_norm%d' % l)
        pre = [self.balloc() for _ in range(4)]
        self.MM_kouter(pre, [('f_w_up', l, 0), ('f_w_up', l, 3072), ('f_w_up', l, 128), ('f_w_up', l, 3072 + 128)])
        self.ffn_pre = {0: (pre[0], pre[1]), 1: (pre[2], pre[3])}
        self.pipeline([self.ffn_chunk(l, j) for j in range(24)], 1)
        for n in range(8):
            if hooks is not None and n in hooks:
                hooks[n]()
            b = self.balloc()
            self.MM(b, [(self.W('mat', 'f_w_down', l, k * 128, n * 128), R('big', S, k, S)) for k in range(24)])
            self.resid_add(n, b)
            self.bfree(b)

    def conformer(self):
        self.norm(self.X, 'c_norm')
        pre = [self.balloc() for _ in range(4)]
        self.MM_kouter(pre, [('c_w_pw1', 0, 0), ('c_w_pw1', 0, 1024), ('c_w_pw1', 0, 128), ('c_w_pw1', 0, 1024 + 128)])
        cpre = {0: (pre[0], pre[1]), 1: (pre[2], pre[3])}
        for c in range(8):
            if c in cpre:
                ba, bb = cpre.pop(c)
            else:
                ba = self.balloc()
                self.MM(ba, [(self.W('mat', 'c_w_pw1', 0, k * 128, c * 128), self.xn(k)) for k in range(8)])
                bb = self.balloc()
                self.MM(bb, [(self.W('mat', 'c_w_pw1', 0, k * 128, 1024 + c * 128), self.xn(k)) for k in range(8)])
            sg = self.talloc()
            self.sigmoid_from(sg, bb, self.PDc('ncb1', c))
            self.bfree(bb)
            self.STT(R('glu', S, c, slice(30, 542)), ba, self.Pc('c_b_pw1', c), sg, ALU.add, ALU.mult)
            self.tfree(sg)
            self.bfree(ba)
        for c in range(8):
            bc = self.balloc()
            self.MM(bc, [(self.W('diag', 'c_dw_w', 0, k, c), R('glu', S, c, slice(k, k + 512))) for k in range(31)])
            bia = self.Pc('c_dw_b', c)
            self.A(AF.Identity, R('f32big', S, c, S), bc, bias=bia)
            self.A(AF.Identity, R('xn', S, c, S), bc, bias=bia)
            self.A(AF.Square, R('sq', S, c, S), bc, bias=bia)
            self.bfree(bc)
            self.CP(R('glu', S, c, slice(0, 30)), R('glu', S, c, slice(512, 542)), eng='pool')
        bm = self.balloc()
        self.MM(bm, [(R('onesD', S, S), R('xn', S, c, S)) for c in range(8)])
        bq = self.balloc()
        self.MM(bq, [(R('onesD', S, S), R('sq', S, c, S)) for c in range(8)])
        msq = self.talloc()
        self.A(AF.Square, msq, bm)
        self.TT(bq, bq, msq, ALU.subtract)
        self.tfree(msq)
        self.A(AF.Ln, bq, bq, bias=EPS)
        self.A(AF.Exp, bq, bq, scale=-0.5)
        def ln_chunk(c):
            t2 = self.talloc()
            self.TT(t2, R('f32big', S, c, S), bm, ALU.subtract)
            yield
            self.TT(t2, t2, bq, ALU.mult)
            yield
            self.A(AF.Silu, R('big', S, c, S), t2, scale=self.Pc('c_ln_g', c), bias=self.Pc('c_ln_b', c))
            self.tfree(t2)
            yield
        self.pipeline([ln_chunk(c) for c in range(8)], 1)
        self.bfree(bm)
        self.bfree(bq)
        for n in range(8):
            b = self.balloc()
            self.MM(b, [(self.W('mat', 'c_w_pw2', 0, k * 128, n * 128), R('big', S, k, S)) for k in range(8)])
            x = self.X(n)
            self.STT(x, b, self.Pc('c_b_pw2', n), x, ALU.add, ALU.add)
            self.bfree(b)

    def trace(self):
        self.xp = 0
        self.setup()
        self.phase = 'tile'
        st = self.stages
        prenormed = False
        for t in range(self.nt):
            self.tile = t
            self.bpos = 0
            self.xp = t % 2
            self.pool_ok = t > 0
            more = t + 1 < self.nt
            if 'mix0' in st:
                self.mixer(t, prenormed)
            if more:
                self.load_x(t + 1)
            if 'xa0' in st:
                self.xattn(0)
            if 'ffn0' in st:
                self.ffn(0)
            if 'conf1' in st:
                self.conformer()
            if 'xa1' in st:
                self.xattn(1)
            prenormed = False
            if 'ffn1' in st:
                hooks = None
                if more and 'mix0' in st:
                    nx = (lambda c, par=(t + 1) % 2: self.X(c, par))
                    hold = {}
                    hooks = {1: (lambda: hold.__setitem__('b', self.norm_stats(nx))),
                             4: (lambda: self.norm_apply(hold['b'], nx, 'ab_norm'))}
                    prenormed = True
                self.ffn(1, hooks)
            if 'final' in st:
                self.norm(self.X, 'final_norm', dst='f32big')
                src = [R('f32big', S, c, S) for c in range(8)]
            else:
                src = [self.X(c) for c in range(8)]
            par = self.xp
            if more and self.known is not None:
                for q in range(NSLOT - 1):
                    self._ensure_chunk(self.ns_pad // CH + (t + 1) * (self.nb_pad // CH) + q, lookahead=0)

            def fn(e, t=t, par=par, final=('final' in st)):
                in_ = self.tm['f32big'][:, :, :] if final else self.tm['x'][:, par, :, :]
                return e.dma_start(out=self.tm['outT'][:, :, t * TT_:(t + 1) * TT_], in_=in_)
            self.P.op('sp', fn, [r.key for r in src], [('outT',)], dma='os')
            if t == 0 and self.known is None:
                self.nb_pad = -(-len(self.tile_blocks) // CH) * CH
        if self.known is None:
            self.ncols = (self.ns_pad + self.nb_pad) * 128

    def emit(self):
        nc = bass.Bass("TRN2", target_bir_lowering=False)
        tm = {}
        self.tm = tm
        seq = self.nt * TT_
        tm['xT'] = nc.dram_tensor("xT", [128, 8, seq], F32, kind="ExternalInput").ap()
        tm['memT'] = nc.dram_tensor("memT", [128, 8, NMEM], F32, kind="ExternalInput").ap()
        tm['wall'] = nc.dram_tensor("wall", [128, self.ncols], F32, kind="ExternalInput").ap()
        tm['params'] = nc.dram_tensor("params", [128, self.np_cols], F32, kind="ExternalInput").ap()
        tm['wbf'] = nc.dram_tensor("wbf", [128, self.ncols], BF16, kind="Internal").ap()
        tm['outT'] = nc.dram_tensor("outT", [128, 8, seq], F32, kind="ExternalOutput").ap()
        sb = [
            ('x', [128, 2, 8, 512], F32), ('f32big', [128, 8, 512], F32),
            ('sq', [128, 8, 512], BF16), ('xn', [128, 8, 512], BF16),
            ('big', [128, 24, 512], BF16), ('glu', [128, 8, 542], BF16),
            ('ring', [128, NSLOT * CH * 128], BF16),
            ('pu', [128, 4, 528], F32), ('psA0', [128, 528], F32), ('psB0', [128, 528], F32),
            ('psA1', [128, 528], F32), ('psB1', [128, 528], F32),
            ('kT', [128, 2, 8, NMEM], BF16), ('v', [128, 2, 2, 1024], BF16),
            ('P', [128, self.np_cols], F32), ('PD', [128, self.npd_cols], F32),
            ('hz', [128, 8, 3], F32), ('hst', [128, 8], F32), ('hf', [128, 2, 24, 2], F32),
            ('onesD', [128, 128], BF16), ('ones1', [128, 128], BF16),
        ]
        sb += [('tmp%d' % i, [128, 512], F32) for i in range(NTMP)]
        sb += [('tb%d' % i, [128, 512], BF16) for i in range(NTB)]
        with ExitStack() as es:
            for name, shape, dt in sb:
                tm[name] = es.enter_context(nc.sbuf_tensor(name, shape, dt))
            for i in range(8):
                tm['ps%d' % i] = es.enter_context(nc.psum_tensor('ps%d' % i, [128, 512], F32))
            sems = {n: es.enter_context(nc.semaphore(n)) for n in self.P.sem_names}
            block = es.enter_context(nc.Block())
            last_os = None
            for o in self.P.ops:
                if o.dma == 'os':
                    last_os = o.comp

            @block.tensor
            def _(e):
                self.P.emit_engine('pe', e, sems)

            @block.scalar
            def _(e):
                self.P.emit_engine('act', e, sems)

            @block.vector
            def _(e):
                self.P.emit_engine('dve', e, sems)

            @block.gpsimd
            def _(e):
                self.P.emit_engine('pool', e, sems)

            @block.sync
            def _(e):
                self.P.emit_engine('sp', e, sems)
                e.wait_ge(sems[last_os[0]], last_os[1])
        return nc

    def host_wall(self, inp):
        wall = np.zeros((128, self.ncols), np.float32)

        def fill(col, spec):
            kind = spec[0]
            if kind == 'mat':
                _, name, l, r0, c0 = spec[:5]
                j = spec[5] if len(spec) > 5 else 0
                blk = inp[name][l][r0:r0 + 128, c0 + j * 128:c0 + (j + 1) * 128]
            elif kind == 'blk':
                _, name, l, h = spec
                blk = inp[name][l][h]
            elif kind == 'diag':
                _, name, l, k, c = spec
                blk = np.diag(inp[name][l][k, c * 128:(c + 1) * 128])
            wall[:, col:col + 128] = blk
        for i, s in enumerate(self.setup_blocks):
            fill(i * 128, s)
        for i, s in enumerate(self.tile_blocks):
            fill((self.ns_pad + i) * 128, s)
        return wall

    def host_params(self, inp):
        Pm = np.zeros((128, self.np_cols), np.float32)

        def put(name, vec):
            vec = np.asarray(vec, np.float32).reshape(-1, 128)
            c = self.pcols[name]
            Pm[:, c:c + vec.shape[0]] = vec.T
        put('ab_norm', inp['ab_norm'][0])
        for k in range(4):
            put('a_conv_w%d' % k, inp['a_conv_w'][0][k])
        for n in ('a_conv_b', 'a_gate_x_b', 'a_gate_a_b', 'a_lambda', 'b_group_b', 'b_scale',
                  'c_norm', 'c_b_pw1', 'c_dw_b', 'c_ln_g', 'c_ln_b', 'c_b_pw2'):
            put(n, inp[n][0])
        for l in range(2):
            put('xa_norm%d' % l, inp['xa_norm'][l])
            put('xa_mem_norm%d' % l, inp['xa_mem_norm'][l])
            put('f_norm%d' % l, inp['f_norm'][l])
            for k in range(3):
                put('f_dw_w%d_%d' % (l, k), inp['f_dw_w'][l][k])
            put('f_dw_b%d' % l, inp['f_dw_b'][l])
        put('final_norm', inp['final_norm'])
        c = self.pcols['pf']
        for g in range(4):
            w = 2 ** (g + 1)
            for t in range(16):
                Pm[:, c + g * 16 + t] = 1.0 / min(t + 1, w)
        return Pm


_CACHE = {}


def get_builder(nt=SEQ // TT_, stages=ALL_STAGES):
    key = (nt, tuple(stages))
    if key not in _CACHE:
        dry = Builder(1, stages)
        _CACHE[key] = Builder(nt, stages, known=(dry.setup_blocks, dry.tile_blocks))
    return _CACHE[key]


def run(inp, nt=SEQ // TT_, stages=ALL_STAGES, ncores=8, trace=False):
    B = get_builder(nt, stages)
    inp = {k: np.asarray(v) for k, v in inp.items()}
    nc = B.emit()
    wall = B.host_wall(inp)
    params = B.host_params(inp)
    seq = nt * TT_
    in_maps = []
    for i in range(ncores):
        xT = np.ascontiguousarray(inp['x'][i, :seq].T.reshape(8, 128, seq).transpose(1, 0, 2))
        memT = np.ascontiguousarray(inp['mem'][i].T.reshape(8, 128, NMEM).transpose(1, 0, 2))
        in_maps.append({'xT': xT, 'memT': memT, 'wall': wall, 'params': params})
    res = run_bass_kernel_spmd(nc, in_maps, core_ids=list(range(ncores)), trace=trace)
    outs = []
    for i in range(ncores):
        o = np.asarray(res.results[i]['outT'])
        outs.append(o.transpose(1, 0, 2).reshape(1024, seq).T)
    return np.stack(outs).astype(np.float32), res


def kernel(**inputs):
    out, _ = run(inputs)
    return out
```
